# Optimizing a Trainium2 kernel written in Bass

```python
import jax, jax.numpy as jnp
from jax import lax
import numpy as np

D_MODEL = 1024
BATCH = 32
SEQ = 256
DEPTH = 1
DEC_BATCH = 4
DEC_SEQ = 2048
PAST_LEN = 512

GRID_W = 64
H_A = 8
DK = 64
DV = 64
W_A = H_A * DV
W_B = 512
CONV_W = 3
CHUNK = 64
D_FF = ((8 * D_MODEL // 3 + 255) // 256) * 256
N_IN = 3 * H_A * DK + W_A + 2 * H_A + 2 * H_A + 3 * W_B + 2 * D_MODEL
SPLITS = tuple(np.cumsum([3 * H_A * DK, W_A, 2 * H_A, 2 * H_A, 3 * W_B]).tolist())
EPS = 1e-6

kernel_name = "hybrid_deltanet_shortconv_diffusion_step"


def rms_norm(x, g):
    xf = x.astype(jnp.float32)
    y = xf * lax.rsqrt(jnp.mean(xf * xf, axis=-1, keepdims=True) + EPS)
    return (y * g.astype(jnp.float32)).astype(x.dtype)


def l2norm(t):
    return t * lax.rsqrt(jnp.sum(t * t, axis=-1, keepdims=True) + EPS)


def conv_centred(x, w, n_rows):
    b, t, ch = x.shape
    xr = x.reshape(b * n_rows, t // n_rows, ch)
    y = lax.conv_general_dilated(xr, w[:, None, :].astype(x.dtype), window_strides=(1,),
                                 padding=((CONV_W // 2, CONV_W // 2),),
                                 dimension_numbers=('NWC', 'WIO', 'NWC'),
                                 feature_group_count=ch)
    return y.reshape(b, t, ch)


def gated_delta_chunked(q, k, v, beta, g, s0):
    b, h, t, _ = q.shape
    n = t // CHUNK
    rs = lambda a: a.reshape(b, h, n, CHUNK, *a.shape[3:])
    q, k, v, beta, g = rs(q), rs(k), rs(v), rs(beta), rs(g)
    g = jnp.cumsum(g, axis=-1)
    tri_incl = jnp.tril(jnp.ones((CHUNK, CHUNK), bool))
    tri_strict = jnp.tril(jnp.ones((CHUNK, CHUNK), bool), -1)
    diff = g[..., :, None] - g[..., None, :]
    decay = jnp.where(tri_incl, jnp.exp(jnp.where(tri_incl, diff, 0.0)), 0.0)
    kb = k * beta[..., None]
    lower = jnp.where(tri_strict, jnp.einsum('bhncd,bhned->bhnce', kb, k) * decay, 0.0)
    eye = jnp.eye(CHUNK, dtype=jnp.float32)
    tmat = lax.linalg.triangular_solve(eye + lower, jnp.broadcast_to(eye, lower.shape),
                                       left_side=True, lower=True)
    u = jnp.einsum('bhnce,bhned->bhncd', tmat, v * beta[..., None])
    w = jnp.einsum('bhnce,bhned->bhncd', tmat, kb * jnp.exp(g)[..., None])
    attn = jnp.where(tri_incl, jnp.einsum('bhncd,bhned->bhnce', q, k) * decay, 0.0)
    g_last = g[..., -1]
    k_tail = k * jnp.exp(g_last[..., None] - g)[..., None]
    q_head = q * jnp.exp(g)[..., None]

    def step(s, xs):
        q_c, w_c, u_c, a_c, k_c, gl = xs
        v_new = u_c - jnp.einsum('bhcd,bhde->bhce', w_c, s)
        o = jnp.einsum('bhcd,bhde->bhce', q_c, s) + jnp.einsum('bhce,bhed->bhcd', a_c, v_new)
        s = s * jnp.exp(gl)[..., None, None] + jnp.einsum('bhcd,bhce->bhde', k_c, v_new)
        return s, o

    mv = lambda a: jnp.moveaxis(a, 2, 0)
    s_fin, o = lax.scan(step, s0, (mv(q_head), mv(w), mv(u), mv(attn), mv(k_tail), mv(g_last)))
    o = jnp.moveaxis(o, 0, 2).reshape(b, h, t, -1)
    return o, s_fin


def mixing(h, s0, n_rows, w_in, conv_qkv, a_log, dt_bias, onorm, conv_sc, w_br_a, w_br_b, w_out):
    b, t, _ = h.shape
    p = h @ w_in
    qkv, z, beta, a, sc, gates = jnp.split(p, SPLITS, axis=-1)
    qkv = jax.nn.silu(conv_centred(qkv, conv_qkv, n_rows))
    q, k, v = jnp.split(qkv, 3, axis=-1)
    heads = lambda x, d: x.reshape(b, t, H_A, d).transpose(0, 2, 1, 3).astype(jnp.float32)
    q = l2norm(heads(q, DK)) * (DK ** -0.5)
    k = l2norm(heads(k, DK))
    v = heads(v, DV)
    dirs = lambda x: x.astype(jnp.float32).reshape(b, t, 2, H_A).transpose(2, 0, 3, 1)
    beta = jax.nn.sigmoid(dirs(beta))
    g = -jnp.exp(a_log.astype(jnp.float32))[:, None, :, None] * jax.nn.softplus(
        dirs(a) + dt_bias.astype(jnp.float32)[:, None, :, None])
    s0 = s0.astype(jnp.float32)
    o_f, s_f = gated_delta_chunked(q, k, v, beta[0], g[0], s0[:, 0])
    fl = lambda x: jnp.flip(x, axis=2)
    o_b, s_b = gated_delta_chunked(fl(q), fl(k), fl(v), fl(beta[1]), fl(g[1]), s0[:, 1])
    o = (o_f + fl(o_b)).transpose(0, 2, 1, 3)
    o = o * lax.rsqrt(jnp.mean(o * o, axis=-1, keepdims=True) + EPS) * onorm.astype(jnp.float32)
    o = (o * jax.nn.silu(z.astype(jnp.float32).reshape(b, t, H_A, DV))).reshape(b, t, W_A)
    y_a = o.astype(h.dtype) @ w_br_a
    bg, cg, xs = jnp.split(sc, 3, axis=-1)
    y_b = (bg * conv_centred(cg * xs, conv_sc, n_rows)) @ w_br_b
    ga, gb = jnp.split(gates, 2, axis=-1)
    m = jax.nn.sigmoid(ga) * y_a + jax.nn.sigmoid(gb) * y_b
    return m @ w_out, jnp.stack([s_f, s_b], axis=1)


def layer(x, mod, s0, n_rows, norm_gains, w_in, conv_qkv, a_log, dt_bias, onorm, conv_sc,
          w_br_a, w_br_b, w_out, w_ffn_in, w_ffn_out):
    sh1, sc1, gt1, sh2, sc2, gt2 = jnp.split(mod[:, None, :], 6, axis=-1)
    h = rms_norm(x, norm_gains[0]) * (1 + sc1) + sh1
    y, s_fin = mixing(h, s0, n_rows, w_in, conv_qkv, a_log, dt_bias, onorm, conv_sc,
                      w_br_a, w_br_b, w_out)
    x = x + gt1 * rms_norm(y, norm_gains[1])
    h = rms_norm(x, norm_gains[2]) * (1 + sc2) + sh2
    gate, up = jnp.split(h @ w_ffn_in, 2, axis=-1)
    y = (jax.nn.silu(gate) * up) @ w_ffn_out
    x = x + gt2 * rms_norm(y, norm_gains[3])
    return x, s_fin


def setup_inputs(seed: int = 0) -> dict:
    key = jax.random.key(seed)
    ks = jax.random.split(key, 20)
    nrm = lambda k, shape, s: jax.random.normal(k, shape, jnp.float32) * s
    dt = jnp.exp(jax.random.uniform(ks[10], (DEPTH, 2, H_A), jnp.float32,
                                    np.log(1e-3), np.log(1e-1)))
    return {
        "x_prompt": nrm(ks[0], (BATCH, SEQ, D_MODEL), 1.0),
        "x_sample": nrm(ks[1], (DEC_BATCH, DEC_SEQ, D_MODEL), 1.0),
        "state_delta": nrm(ks[2], (DEC_BATCH, DEPTH, 2, H_A, DK, DV), 0.05),
        "c": nrm(ks[3], (DEC_BATCH, D_MODEL), 1.0),
        "c_ctx": nrm(ks[4], (D_MODEL,), 1.0),
        "w_mod": nrm(ks[5], (DEPTH, D_MODEL, 6 * D_MODEL), 0.5 * D_MODEL ** -0.5),
        "b_mod": nrm(ks[6], (DEPTH, 6 * D_MODEL), 0.02),
        "norm_gains": 1.0 + nrm(ks[7], (DEPTH, 4, D_MODEL), 0.02),
        "w_in": nrm(ks[8], (DEPTH, D_MODEL, N_IN), D_MODEL ** -0.5),
        "conv_qkv": nrm(ks[9], (DEPTH, CONV_W, 3 * H_A * DK), CONV_W ** -0.5),
        "a_log": jnp.log(jax.random.uniform(ks[11], (DEPTH, 2, H_A), jnp.float32, 1.0, 16.0)),
        "dt_bias": dt + jnp.log(-jnp.expm1(-dt)),
        "onorm": 1.0 + nrm(ks[12], (DEPTH, DV), 0.02),
        "conv_sc": nrm(ks[13], (DEPTH, CONV_W, W_B), CONV_W ** -0.5),
        "w_branch_a": nrm(ks[14], (DEPTH, W_A, D_MODEL), W_A ** -0.5),
        "w_branch_b": nrm(ks[15], (DEPTH, W_B, D_MODEL), W_B ** -0.5),
        "w_out": nrm(ks[16], (DEPTH, D_MODEL, D_MODEL), D_MODEL ** -0.5),
        "w_ffn_in": nrm(ks[17], (DEPTH, D_MODEL, 2 * D_FF), D_MODEL ** -0.5),
        "w_ffn_out": nrm(ks[18], (DEPTH, D_FF, D_MODEL), D_FF ** -0.5),
    }


def reference(x_prompt, x_sample, state_delta, c, c_ctx, w_mod, b_mod, norm_gains, w_in,
              conv_qkv, a_log, dt_bias, onorm, conv_sc, w_branch_a, w_branch_b, w_out,
              w_ffn_in, w_ffn_out):
    rows = x_sample.shape[1] // GRID_W
    y_p = x_prompt
    s_zero = jnp.zeros((x_prompt.shape[0], 2, H_A, DK, DV), jnp.float32)
    new_states = []
    for l in range(DEPTH):
        mod_ctx = (jax.nn.silu(c_ctx) @ w_mod[l] + b_mod[l])[None, :]
        y_p, s_fin = layer(y_p, mod_ctx, s_zero, 1, norm_gains[l], w_in[l], conv_qkv[l],
                           a_log[l], dt_bias[l], onorm[l], conv_sc[l], w_branch_a[l],
                           w_branch_b[l], w_out[l], w_ffn_in[l], w_ffn_out[l])
        new_states.append(s_fin)
    new_state_delta = jnp.stack(new_states, axis=1).astype(x_prompt.dtype)
    y_s = x_sample
    for l in range(DEPTH):
        mod_lat = jax.nn.silu(c) @ w_mod[l] + b_mod[l]
        y_s, _ = layer(y_s, mod_lat, state_delta[:, l], rows, norm_gains[l], w_in[l],
                       conv_qkv[l], a_log[l], dt_bias[l], onorm[l], conv_sc[l],
                       w_branch_a[l], w_branch_b[l], w_out[l], w_ffn_in[l], w_ffn_out[l])
    return (y_p, y_s, new_state_delta)
```

```python
import numpy as np
from concourse.bass_utils import run_bass_kernel_spmd
import concourse.bass as bass
import concourse.mybir as mybir
from contextlib import ExitStack

F32 = mybir.dt.float32
BF16 = mybir.dt.bfloat16
AF = mybir.ActivationFunctionType
ALU = mybir.AluOpType
AX = mybir.AxisListType


class Buf:
    __slots__ = ("name", "w", "r", "dsem", "dcnt")

    def __init__(self, name):
        self.name = name
        self.w = []
        self.r = []
        self.dsem = None
        self.dcnt = 0


class KB:
    COMPUTE = ("pe", "act", "dve", "pool")
    NAMES = {}

    def __init__(self, nc, es, strict_same=True):
        self.nc = nc
        self.es = es
        self.strict_same = strict_same
        self.eng = dict(pe=nc.tensor, act=nc.scalar, dve=nc.vector, pool=nc.gpsimd, sp=nc.sync)
        self.q = {e: [] for e in self.eng}
        self.sem = {e: es.enter_context(nc.semaphore("c_" + e)) for e in self.COMPUTE}
        self.cnt = {e: 0 for e in self.COMPUTE}
        self.waited = {e: {} for e in self.eng}
        self.pending_noinc = {e: False for e in self.COMPUTE}
        self.nsem = 4
        self.all_dma_bufs = []

    def buf(self, name):
        return Buf(name)

    def bufs(self, name, n):
        return [Buf(f"{name}{i}") for i in range(n)]

    def _waits(self, e, reads, writes, nosame=False, skip_sem=None):
        deps = []
        for b in reads:
            deps += b.w
        for b in writes:
            deps += b.w
            deps += b.r
        need = {}
        for (s, v, owner) in deps:
            if owner == e and (nosame or not self.strict_same or e == "pe"):
                continue
            if skip_sem is not None and s.num == skip_sem:
                continue
            if self.waited[e].get(s.num, 0) >= v:
                continue
            if need.get(s.num, (None, 0))[1] < v:
                need[s.num] = (s, v)
        out = []
        for num, (s, v) in need.items():
            self.waited[e][num] = v
            out.append((s, v))
        return out

    def op(self, e, fn, reads=(), writes=(), inc=True, nosame=False):
        waits = self._waits(e, reads, writes, nosame=nosame)
        sem = self.sem[e]
        import sys as _sys
        _fr = _sys._getframe(1)
        while _fr is not None and _fr.f_code.co_name in ("act", "tt", "stt", "ts", "cp", "mm", "op"):
            _fr = _fr.f_back
        _ln = _fr.f_lineno if _fr is not None else -1
        if inc:
            self.cnt[e] += 1
            tokv = self.cnt[e]
            self.pending_noinc[e] = False
        else:
            tokv = self.cnt[e] + 1
            self.pending_noinc[e] = True
        tok = (sem, tokv, e)

        def emit(engine, waits=waits, fn=fn, inc=inc, sem=sem, _ln=_ln):
            for (s, v) in waits:
                engine.wait_ge(s, v)
            ins = fn(engine)
            try:
                KB.NAMES[str(ins.ins.name)] = _ln
            except Exception:
                pass
            if inc:
                ins.then_inc(sem, 1)
        self.q[e].append(emit)
        for b in writes:
            b.w = [tok]
            b.r = []
        for b in reads:
            if b not in writes:
                b.r.append(tok)
        return tok

    def dma(self, qe, out, in_, reads=(), writes=(), track=None, partial=False):
        tb = track if track is not None else (writes[0] if writes else reads[0])
        waits = self._waits(qe, reads, writes,
                            skip_sem=(tb.dsem.num if (partial and tb.dsem is not None) else None))
        if tb.dsem is None:
            tb.dsem = self.es.enter_context(self.nc.semaphore(f"d{self.nsem}_" + tb.name))
            self.nsem += 1
            self.all_dma_bufs.append(tb)
        tb.dcnt += 16
        tok = (tb.dsem, tb.dcnt, "dma")

        def emit(engine, waits=waits, out=out, in_=in_, s=tb.dsem):
            for (sm, v) in waits:
                engine.wait_ge(sm, v)
            engine.dma_start(out=out, in_=in_).then_inc(s, 16)
        self.q[qe].append(emit)
        for b in writes:
            b.w = [tok]
            b.r = []
        for b in reads:
            if b not in writes:
                b.r.append(tok)
        return tok

    def wait_tokens(self, e, toks):
        need = []
        for (s, v, owner) in toks:
            if self.waited[e].get(s.num, 0) >= v:
                continue
            self.waited[e][s.num] = v
            need.append((s, v))

        def emit(engine, need=need):
            for (s, v) in need:
                engine.wait_ge(s, v)
        self.q[e].append(emit)

    def barrier(self):
        toks = [(self.sem[e], self.cnt[e], e) for e in self.COMPUTE if self.cnt[e] > 0]
        toks += [(b.dsem, b.dcnt, "dma") for b in self.all_dma_bufs]
        for e in self.eng:
            assert not self.pending_noinc.get(e, False)
            self.wait_tokens(e, toks)

    def flush(self):
        for e in self.COMPUTE:
            assert not self.pending_noinc[e], f"engine {e} ends with a non-incrementing op"
        q = self.q
        with self.nc.Block() as block:
            @block.sync
            def _(eng):
                for f in q["sp"]:
                    f(eng)

            @block.tensor
            def _(eng):
                for f in q["pe"]:
                    f(eng)

            @block.scalar
            def _(eng):
                for f in q["act"]:
                    f(eng)

            @block.vector
            def _(eng):
                for f in q["dve"]:
                    f(eng)

            @block.gpsimd
            def _(eng):
                for f in q["pool"]:
                    f(eng)
        self.q = {e: [] for e in self.eng}

    def finish(self):
        self.flush()


DM = 1024
TOK = 2048
NT = 16
NTT = 4
N_IN = 5664
DFF = 2816
Z0, BA0, SC0, CG0, XS0, GA0, GB0 = 1536, 2048, 2080, 2592, 3104, 3616, 4640
EPS = 1e-6
IDENT, MFI, MFS, MBI, MBS, BONES, ONES, BD32, M1I, M2I, NEGTF, NEGTB = range(12)
NMAT = 12


def build_program(stage=99):
    nc = bass.Bass("TRN2", target_bir_lowering=False)
    di = lambda n, s: nc.dram_tensor(n, s, F32, kind="ExternalInput").ap()
    x = di("x", [TOK, DM])
    cvec = di("cvec", [128, 8])
    flags = di("flags", [128, 2])
    init_s = di("init_s", [2, 8, 4, 128, 64])
    w_mod = di("w_mod", [DM, 6 * DM])
    b_mod = di("b_mod", [6 * DM])
    norm_gains = di("norm_gains", [4, DM])
    w_in = di("w_in", [DM, N_IN])
    cwq_d = di("cwq", [128, 12, 3])
    cws_d = di("cws", [128, 4, 3])
    a_log = di("a_log", [16])
    dt_bias = di("dt_bias", [16])
    onorm = di("onorm", [64])
    w_br_a = di("w_br_a", [512, DM])
    w_br_b = di("w_br_b", [512, DM])
    w_out = di("w_out", [DM, DM])
    w_ffn_in = di("w_ffn_in", [DM, 2 * DFF])
    w_ffn_out = di("w_ffn_out", [DFF, DM])
    cmat = di("cmat", [NMAT, 128, 128])
    y = nc.dram_tensor("y", [TOK, DM], F32, kind="ExternalOutput").ap()
    s_out = nc.dram_tensor("s_out", [2, 8, 4, 128, 64], F32, kind="ExternalOutput").ap()
    dbg = None
    if stage < 99:
        dbg = nc.dram_tensor("dbg", [128, 16384], F32, kind="ExternalOutput").ap()

    with ExitStack() as es:
        kb = KB(nc, es)
        T = lambda n, shape, dt=F32: es.enter_context(nc.sbuf_tensor(n, shape, dt))
        op = kb.op

        def act(out, in_, func, reads, writes, **kw):
            return op("act", lambda e: e.activation(out=out, in_=in_, func=func, **kw), reads, writes)

        def tt(eng, out, in0, in1, alu, reads, writes):
            return op(eng, lambda e: e.tensor_tensor(out=out, in0=in0, in1=in1, op=alu), reads, writes)

        def stt(eng, out, in0, scalar, in1, op0, op1, reads, writes):
            return op(eng, lambda e: e.scalar_tensor_tensor(out=out, in0=in0, scalar=scalar, in1=in1,
                                                            op0=op0, op1=op1), reads, writes)

        def ts(eng, out, in0, s1, s2, op0, op1, reads, writes):
            if s2 is None:
                return op(eng, lambda e: e.tensor_scalar(out=out, in0=in0, scalar1=s1, scalar2=None, op0=op0),
                          reads, writes)
            return op(eng, lambda e: e.tensor_scalar(out=out, in0=in0, scalar1=s1, scalar2=s2, op0=op0, op1=op1),
                      reads, writes)

        def cp(eng, out, in_, reads, writes):
            if eng == "act":
                return act(out, in_, AF.Copy, reads, writes)
            return op(eng, lambda e: e.tensor_copy(out=out, in_=in_), reads, writes)

        def mm(out, lhsT, rhs, start, stop, reads, writes, inc=None):
            return op("pe", lambda e: e.matmul(out, lhsT=lhsT, rhs=rhs, start=start, stop=stop),
                      reads, writes, inc=(stop if inc is None else inc))

        def prefetched(loaders, depth=1):
            issued = []

            def get(i):
                while len(issued) < min(len(loaders), i + 1 + depth):
                    issued.append(loaders[len(issued)]())
                return issued[i]
            return get

        def pipeline(gens, depth):
            gens = iter(gens)
            active = []
            done = False
            while True:
                while not done and len(active) < depth:
                    try:
                        active.append(next(gens))
                    except StopIteration:
                        done = True
                if not active:
                    break
                for g in list(active):
                    try:
                        next(g)
                    except StopIteration:
                        active.remove(g)

        dbg_state = dict(off=0, tok=None)
        if stage < 99:
            dstage = T("dstage", [128, 1024]); b_dstage = kb.buf("dstage"); b_dbg = kb.buf("dbgout")

        def dump(ap, width, reads):
            view = dstage[:, 0:width]
            if len(ap.shape) == 3:
                view = view.rearrange("p (a b) -> p a b", a=ap.shape[1])
            cp("dve", view, ap, reads, [b_dstage])
            o = dbg_state["off"]
            dbg_state["tok"] = kb.dma("sp", dbg[:, o:o + width], dstage[:, 0:width], reads=[b_dstage], track=b_dbg)
            dbg_state["off"] = o + width

        def dump_done():
            kb.wait_tokens("sp", [dbg_state["tok"]])
            kb.finish()

        PS = es.enter_context(nc.psum_tensor("ps", [128, 8, 512], F32))
        pb = kb.bufs("psb", 8)

        cm_b = T("cm_b", [128, NMAT, 128], BF16); b_cmb = kb.buf("cmb")
        flg = T("flg", [128, 2]); b_flg = kb.buf("flg")
        eps_t = T("eps_t", [128, 3]); b_eps = kb.buf("eps")
        NWB = 3 if stage >= 99 else 2
        KB_ = 1024
        AR_BYTES = (203 if stage >= 99 else 199) * KB_
        AR = T("arena", [128, AR_BYTES // 2], BF16)

        class Region:
            def __init__(self, start, end):
                self.start, self.end, self.pos = start, end, start

            def take(self, shape, dt=F32):
                n = 1
                for s_ in shape[1:]:
                    n *= s_
                nbytes = n * (4 if dt == F32 else 2)
                nbytes = (nbytes + 63) // 64 * 64
                assert self.pos + nbytes <= self.end, (self.pos, nbytes, self.end, shape)
                v = AR[:, self.pos // 2:(self.pos + n * (4 if dt == F32 else 2)) // 2]
                self.pos += nbytes
                if dt == F32:
                    v = v.bitcast(F32)
                if len(shape) == 3:
                    v = v.rearrange("p (a b) -> p a b", a=shape[1])
                elif len(shape) == 4:
                    v = v.rearrange("p (a b c) -> p a b c", a=shape[1], b=shape[2])
                return v

        RP = Region(0, 40 * KB_)
        RA = Region(40 * KB_, 88 * KB_)
        RB = Region(88 * KB_, 104 * KB_)
        RC = Region(104 * KB_, 110 * KB_)
        D0 = 110 * KB_
        modr = RP.take([128, 6, DM]); b_modr = kb.bufs("modr", 6)
        WBH = {}
        b_wb = kb.bufs("wb", 3)
        wb_ctr = [0]

        def wload(src_ap, ncols=512, parts=None):
            WB = WBH['wb']
            i = wb_ctr[0] % len(WB)
            wb_ctr[0] += 1
            if parts is None:
                nk_ = src_ap.shape[0] // 128
                kb.dma("pool", WB[i][:, 0:nk_, 0:ncols], src_ap.rearrange("(kc p) c -> p kc c", p=128), writes=[b_wb[i]])
            else:
                off = 0
                for k, (ap, w) in enumerate(parts):
                    kb.dma("pool", WB[i][:, :, off:off + w], ap.rearrange("(kc p) c -> p kc c", p=128),
                           writes=[b_wb[i]], partial=(k > 0))
                    off += w
            return WB[i], b_wb[i]

        identb = cm_b[:, IDENT, :]
        idf = T("idf", [128, 128]); b_idf = kb.buf("idf")
        kb.dma("sp", idf[:], cmat[IDENT], writes=[b_idf])

        kb.dma("pool", cm_b[:], cmat.rearrange("m p c -> p m c"), writes=[b_cmb])
        kb.dma("sp", flg[:], flags, writes=[b_flg])
        op("dve", lambda e: e.memset(eps_t[:, 0:1], EPS), writes=[b_eps])
        op("dve", lambda e: e.memset(eps_t[:, 1:2], 1.0), writes=[b_eps])
        op("dve", lambda e: e.memset(eps_t[:, 2:3], 64.0 * EPS), writes=[b_eps])

        ysc = RP.take([128, 4, TOK], BF16); b_ysc = kb.bufs("ysc", 16)
        T12 = lambda n, shape, dt=F32: RC.take(shape, dt)
        qT = RA.take([128, 4, TOK], BF16); b_qT = kb.bufs("qT", 16)
        kT = RA.take([128, 4, TOK], BF16); b_kT = kb.bufs("kT", 16)
        v_tok = RA.take([128, NT, 512], BF16); b_vt = kb.bufs("vt", 16)
        zs = RB.take([128, NT, 512], BF16); b_zs = kb.bufs("zs", 16)
        bet = T12("bet", [128, NT, 16]); b_bet = kb.buf("bet")
        nbet = T12("nbet", [128, NT, 16]); b_nbet = kb.buf("nbet")
        gg = T12("gg", [128, NT, 16]); b_gg = kb.buf("gg")
        gbf = T12("gbf", [128, NT, 16], BF16); b_gbf = kb.buf("gbf")
        egc = T12("egc", [128, NT, 16]); b_egc = kb.buf("egc")
        ekt = T12("ekt", [128, NT, 16]); b_ekt = kb.buf("ekt")
        eglp = T12("eglp", [128, NT, 2, 4]); b_eglp = kb.buf("eglp")
        cwq = T("cwq_s", [128, 12, 3]); b_cwq = kb.buf("cwq")
        cwqp = T("cwqp", [128, 12, 3]); b_cwqp = kb.buf("cwqp")
        cws = T("cws_s", [128, 4, 3]); b_cws = kb.buf("cws")
        cwsp = T("cwsp", [128, 4, 3]); b_cwsp = kb.buf("cwsp")
        kb.dma("sp", cwq[:], cwq_d, writes=[b_cwq])
        kb.dma("sp", cws[:], cws_d, writes=[b_cws])
        ts("dve", cwqp[:], cwq[:], flg[:, 1:2], None, ALU.mult, None, [b_cwq, b_flg], [b_cwqp])
        ts("dve", cwsp[:], cws[:], flg[:, 1:2], None, ALU.mult, None, [b_cws, b_flg], [b_cwsp])

        RD = Region(D0, AR_BYTES)
        WBH['wb'] = [RD.take([128, 8, 512], BF16) for i in range(NWB)]
        D1 = RD.pos
        if True:
            R0 = Region(D1, AR_BYTES)
            T0 = lambda n, shape, dt=F32: R0.take(shape, dt)
            c_sb = T0("c_sb", [128, 8]); b_c = kb.buf("c_sb")
            cs = T0("cs", [128, 8]); b_cs = kb.buf("cs")
            scb = T0("scb", [128, 8, 128], BF16); b_scb = kb.buf("scb")
            bmr = T0("bmr", [128, 2, DM]); b_bmr = kb.bufs("bmr", 2)
            gr = T0("gr", [128, 4, DM]); b_gr = kb.bufs("gr", 4)
            tmpm = T0("tmpm", [128, DM]); b_tmpm = kb.buf("tmpm")
            kb.dma("sp", c_sb[:], cvec, writes=[b_c])
            for i in range(4):
                kb.dma("sp", gr[:, i, :], norm_gains[i].partition_broadcast(128), writes=[b_gr[i]])
            act(cs[:], c_sb[:], AF.Silu, [b_c], [b_cs])
            cp("dve", scb[:], cs[:].unsqueeze(2).to_broadcast([128, 8, 128]), [b_cs], [b_scb])
            for s in range(6):
                bs = s % 2
                kb.dma("sp", bmr[:, bs, :], b_mod[s * DM:(s + 1) * DM].partition_broadcast(128), writes=[b_bmr[bs]])
                for nb in range(2):
                    wt, bw = wload(w_mod[:, s * DM + nb * 512: s * DM + (nb + 1) * 512])
                    bank = 2 * bs + nb
                    for kc in range(8):
                        mm(PS[:, bank, :], scb[:, kc, :], wt[:, kc, :], kc == 0, kc == 7, [b_scb, bw], [pb[bank]])
                ps2 = PS[:, 2 * bs:2 * bs + 2, :].rearrange("p a b -> p (a b)")
                rd = [pb[2 * bs], pb[2 * bs + 1], b_bmr[bs]]
                kind = s % 3
                slot = [0, 1, 2, 3, 4, 5][s]
                if kind == 0:
                    tt("dve", modr[:, slot, :], ps2, bmr[:, bs, :], ALU.add, rd, [b_modr[slot]])
                elif kind == 1:
                    gi = 0 if s == 1 else 2
                    tt("dve", tmpm[:], ps2, bmr[:, bs, :], ALU.add, rd, [b_tmpm])
                    stt("dve", modr[:, slot, :], tmpm[:], 1.0, gr[:, gi, :], ALU.add, ALU.mult,
                        [b_tmpm, b_gr[gi]], [b_modr[slot]])
                else:
                    gi = 1 if s == 2 else 3
                    tt("dve", tmpm[:], ps2, bmr[:, bs, :], ALU.add, rd, [b_tmpm])
                    tt("dve", modr[:, slot, :], tmpm[:], gr[:, gi, :], ALU.mult, [b_tmpm, b_gr[gi]], [b_modr[slot]])
            kb.barrier()
            kb.flush()
        B1r, A1r, C1r, B2r, A2r, C2r = [modr[:, i, :] for i in range(6)]

        if stage == 0:
            kb.barrier()
            for i in range(6):
                dump(modr[:, i, :], 1024, [b_modr[i]])
            dump_done()
            return nc


        if True:
            R1 = Region(D1, AR_BYTES)
            T1 = lambda n, shape, dt=F32: R1.take(shape, dt)
            hT = T1("hT", [128, 8, TOK], BF16); b_hT = kb.bufs("hT", 16)
            xst = [T1(f"xst{i}", [128, DM]) for i in range(2)]; b_xst = kb.bufs("xst", 2)
            ssq = T1("ssq", [128, NT]); b_ssq = kb.bufs("ssq", 16)
            sdv = T1("sdv", [128, NT]); b_sdv = kb.bufs("sdv", 16)
            rstd = T1("rstd", [128, NT]); b_rstd = kb.bufs("rstd", 16)
            t1 = [T1("t1_0", [128, DM])] * 2; b_t1 = [kb.buf("t1_")] * 2
            htok = [T1(f"htok{i}", [128, DM], BF16) for i in range(2)]; b_htok = kb.bufs("htok", 2)
            wba = T1("wba", [128, 8, 32], BF16); b_wba = kb.buf("wba")
            dtb = T1("dtb", [128, 16]); b_dtb = kb.buf("dtb")
            alr = T1("alr", [128, 16]); b_alr = kb.buf("alr")
            nA = T1("nA", [128, 16]); b_nA = kb.buf("nA")
            sp = [T1(f"sp{i}", [128, NT, 16]) for i in range(4)]; b_sp = kb.bufs("sp", 4)
            gcs = T1("gcs", [128, NT, 32]); b_gcs = kb.buf("gcs")
            ycv = [T1(f"ycv{i}", [128, 512]) for i in range(2)]; b_ycv = kb.bufs("ycv", 2)
            ysb = [T1(f"ysb{i}", [128, 512], BF16) for i in range(2)]; b_ysb = kb.bufs("ysb", 2)
            sqb = [T1(f"sqb{i}", [128, 512], BF16) for i in range(2)]; b_sqb = kb.bufs("sqb", 2)
            sdn = [T1(f"sdn{i}", [128, 512]) for i in range(2)]; b_sdn = kb.bufs("sdn", 2)
            xss = sdn; b_xss = b_sdn

            kb.dma("pool", wba[:], w_in[:, BA0:BA0 + 32].rearrange("(kc p) c -> p kc c", p=128), writes=[b_wba])
            kb.dma("sp", dtb[:], dt_bias.partition_broadcast(128), writes=[b_dtb])
            kb.dma("sp", alr[:], a_log.partition_broadcast(128), writes=[b_alr])

            def it_1a(n):
                i2 = n % 2
                kb.dma("sp", xst[i2][:], x[n * 128:(n + 1) * 128, :], writes=[b_xst[i2]])
                act(htok[i2][:], xst[i2][:], AF.Square, [b_xst[i2]], [b_htok[i2], b_ssq[n]], accum_out=ssq[:, n:n + 1])
                act(sdv[:, n:n + 1], ssq[:, n:n + 1], AF.Sqrt, [b_ssq[n], b_eps], [b_sdv[n]],
                    scale=1.0 / DM, bias=eps_t[:, 0:1])
                yield
                op("dve", lambda e, n=n: e.reciprocal(out=rstd[:, n:n + 1], in_=sdv[:, n:n + 1]), [b_sdv[n]], [b_rstd[n]])
                stt("dve", xst[i2][:], xst[i2][:], rstd[:, n:n + 1], A1r, ALU.mult, ALU.mult,
                    [b_xst[i2], b_rstd[n], b_modr[1]], [b_xst[i2]])
                yield
                tt("dve", htok[i2][:], xst[i2][:], B1r, ALU.add, [b_xst[i2], b_modr[0]], [b_htok[i2]])
                yield
                bk = 2 * i2
                for kc in range(8):
                    mm(PS[:, bk + kc // 4, (kc % 4) * 128:(kc % 4 + 1) * 128], htok[i2][:, kc * 128:(kc + 1) * 128], identb,
                       True, True, [b_htok[i2], b_cmb], [pb[bk + kc // 4]], inc=(kc % 4 == 3))
                yield
                cp("act", hT[:, :, n * 128:(n + 1) * 128],
                   PS[:, bk:bk + 2, :].rearrange("p a (c t) -> p (a c) t", t=128), [pb[bk], pb[bk + 1]], [b_hT[n]])
                yield
            pipeline((it_1a(n) for n in range(NT)), 2)

            if stage == 1:
                kb.barrier()
                for kc in range(8):
                    dump(hT[:, kc, 0:1024], 1024, b_hT)
                dump_done()
                return nc

            wz, b_wz = wload(w_in[:, Z0:Z0 + 512])
            for n in range(NT):
                bk = n % 2
                for kc in range(8):
                    mm(PS[:, bk, :], hT[:, kc, n * 128:(n + 1) * 128], wz[:, kc, :], kc == 0, kc == 7,
                       [b_hT[n], b_wz], [pb[bk]])
                act(zs[:, n, :], PS[:, bk, :], AF.Silu, [pb[bk]], [b_zs[n]])
                for kc in range(8):
                    mm(PS[:, 7, n * 32:(n + 1) * 32], hT[:, kc, n * 128:(n + 1) * 128], wba[:, kc, :], kc == 0, kc == 7,
                       [b_hT[n], b_wba], [pb[7]])
            pba = PS[:, 7, :].rearrange("p (n c) -> p n c", c=32)
            a1, ab, e1, r1 = sp
            act(bet[:], pba[:, :, 0:16], AF.Sigmoid, [pb[7]], [b_bet])
            ts("dve", nbet[:], bet[:], -1.0, None, ALU.mult, None, [b_bet], [b_nbet])
            tt("dve", a1[:], pba[:, :, 16:32], dtb[:].unsqueeze(1).to_broadcast([128, NT, 16]), ALU.add,
               [pb[7], b_dtb], [b_sp[0]])
            act(ab[:], a1[:], AF.Abs, [b_sp[0]], [b_sp[1]])
            act(e1[:], ab[:], AF.Exp, [b_sp[1]], [b_sp[2]], scale=-1.0)
            act(e1[:], e1[:], AF.Ln, [b_sp[2], b_eps], [b_sp[2]], bias=eps_t[:, 1:2])
            ts("dve", r1[:], a1[:], 0.0, None, ALU.max, None, [b_sp[0]], [b_sp[3]])
            tt("dve", r1[:], r1[:], e1[:], ALU.add, [b_sp[3], b_sp[2]], [b_sp[3]])
            act(nA[:], alr[:], AF.Exp, [b_alr], [b_nA])
            ts("dve", nA[:], nA[:], -1.0, None, ALU.mult, None, [b_nA], [b_nA])
            tt("dve", gg[:], r1[:], nA[:].unsqueeze(1).to_broadcast([128, NT, 16]), ALU.mult, [b_sp[3], b_nA], [b_gg])
            cp("dve", gbf[:], gg[:], [b_gg], [b_gbf])
            pgc = PS[:, 6, :].rearrange("p (n c) -> p n c", c=32)
            for n in range(NT):
                mm(pgc[:, n, 0:8], cm_b[:, MFI, :], gbf[:, n, 0:8], True, True, [b_cmb, b_gbf], [pb[6]], inc=False)
                mm(pgc[:, n, 8:16], cm_b[:, MBI, :], gbf[:, n, 8:16], True, True, [b_cmb, b_gbf], [pb[6]], inc=False)
                mm(pgc[:, n, 16:32], cm_b[:, ONES, :], gbf[:, n, 0:16], True, True, [b_cmb, b_gbf], [pb[6]],
                   inc=(n == NT - 1))
            cp("act", gcs[:], pgc, [pb[6]], [b_gcs])
            act(egc[:], gcs[:, :, 0:16], AF.Exp, [b_gcs], [b_egc])
            tt("dve", ab[:], gcs[:, :, 16:32], gcs[:, :, 0:16], ALU.subtract, [b_gcs], [b_sp[1]])
            act(ekt[:], ab[:], AF.Exp, [b_sp[1]], [b_ekt])
            act(a1[:], gcs[:, :, 16:32], AF.Exp, [b_gcs], [b_sp[0]])
            a5 = a1[:].rearrange("p n (d q h) -> p n d q h", d=2, q=4)
            cp("dve", eglp[0:64], a5[0:64, :, :, :, 0], [b_sp[0]], [b_eglp])
            cp("dve", eglp[64:128], a5[64:128, :, :, :, 1], [b_sp[0]], [b_eglp])

            if stage == 15:
                kb.barrier()
                dump(zs[:, 14:16, :], 1024, b_zs)
                dump(bet[:], 256, [b_bet])
                dump(gg[:], 256, [b_gg])
                dump(egc[:], 256, [b_egc])
                dump(ekt[:], 256, [b_ekt])
                dump(eglp[:].rearrange("p n d q -> p n (d q)"), 128, [b_eglp])
                dump_done()
                return nc

            def conv_g(src, dst, bdst, cwt, cwpt, fc, rd, eng):
                act(dst[:], src, AF.Copy, rd + [b_cwq, b_cws], [bdst], scale=cwt[:, fc, 1:2])
                yield
                d3 = dst[:].rearrange("p (s c) -> p s c", c=64)
                s3 = src.rearrange("p (s c) -> p s c", c=64)
                stt(eng, d3[:, :, 1:64], s3[:, :, 0:63], cwt[:, fc, 0:1], d3[:, :, 1:64], ALU.mult, ALU.add,
                    rd + [bdst], [bdst])
                stt(eng, d3[:, :, 0:63], s3[:, :, 1:64], cwt[:, fc, 2:3], d3[:, :, 0:63], ALU.mult, ALU.add,
                    rd + [bdst], [bdst])
                d4 = dst[:].rearrange("p (q s c) -> p q s c", q=2, s=4)
                s4 = src.rearrange("p (q s c) -> p q s c", q=2, s=4)
                stt(eng, d4[:, :, 1:4, 0:1], s4[:, :, 0:3, 63:64], cwpt[:, fc, 0:1], d4[:, :, 1:4, 0:1],
                    ALU.mult, ALU.add, rd + [bdst, b_cwqp, b_cwsp], [bdst])
                stt(eng, d4[:, :, 0:3, 63:64], s4[:, :, 1:4, 0:1], cwpt[:, fc, 2:3], d4[:, :, 0:3, 63:64],
                    ALU.mult, ALU.add, rd + [bdst], [bdst])
                yield

            def it_qkv(blk, c, t4, rot, wt, bw):
                fc = blk * 4 + c
                bank = rot % 4
                i2 = rot % 2
                tsl = slice(t4 * 512, (t4 + 1) * 512)
                for kc in range(8):
                    mm(PS[:, bank, :], wt[:, kc, c * 128:(c + 1) * 128], hT[:, kc, tsl], kc == 0, kc == 7,
                       [bw] + b_hT[4 * t4:4 * t4 + 4], [pb[bank]])
                yield
                for _ in conv_g(PS[:, bank, :], ycv[i2], b_ycv[i2], cwq, cwqp, fc, [pb[bank]], "dve"):
                    yield
                if blk == 2:
                    act(ysb[i2][:], ycv[i2][:], AF.Silu, [b_ycv[i2]], [b_ysb[i2]])
                    yield
                    pbk = 4 + i2
                    for j in range(4):
                        mm(PS[:, pbk, j * 128:(j + 1) * 128], ysb[i2][:, j * 128:(j + 1) * 128], identb,
                           True, True, [b_ysb[i2], b_cmb], [pb[pbk]], inc=(j == 3))
                    yield
                    cp("act", v_tok[:, t4 * 4:(t4 + 1) * 4, c * 128:(c + 1) * 128],
                       PS[:, pbk, :].rearrange("p (j f) -> p j f", f=128), [pb[pbk]], b_vt[4 * t4:4 * t4 + 4])
                    yield
                else:
                    act(ycv[i2][:], ycv[i2][:], AF.Silu, [b_ycv[i2]], [b_ycv[i2]])
                    yield
                    tt("pool", sqb[i2][:], ycv[i2][:], ycv[i2][:], ALU.mult, [b_ycv[i2]], [b_sqb[i2]])
                    yield
                    pbk = 6 + i2
                    mm(PS[:, pbk, :], cm_b[:, BONES, :], sqb[i2][:], True, True, [b_cmb, b_sqb[i2]], [pb[pbk]])
                    yield
                    if blk == 0:
                        act(sdn[i2][:], PS[:, pbk, :], AF.Sqrt, [pb[pbk], b_eps], [b_sdn[i2]], scale=64.0, bias=eps_t[:, 2:3])
                    else:
                        act(sdn[i2][:], PS[:, pbk, :], AF.Sqrt, [pb[pbk], b_eps], [b_sdn[i2]], bias=eps_t[:, 0:1])
                    yield
                    op("dve", lambda e, i2=i2: e.reciprocal(out=sdn[i2][:], in_=sdn[i2][:]),
                       [b_sdn[i2]], [b_sdn[i2]])
                    yield
                    dstT, bdT = (qT, b_qT) if blk == 0 else (kT, b_kT)
                    tt("pool", dstT[:, c, tsl], ycv[i2][:], sdn[i2][:], ALU.mult, [b_ycv[i2], b_sdn[i2]], bdT[4 * t4:4 * t4 + 4])
                    yield

            def gen_qkv():
                rot = 0
                getw = prefetched([(lambda blk=blk: wload(w_in[:, blk * 512:(blk + 1) * 512])) for blk in range(3)])
                for blk in range(3):
                    wt, bw = getw(blk)
                    for c in range(4):
                        for t4 in range(NTT):
                            yield it_qkv(blk, c, t4, rot, wt, bw)
                            rot += 1
            pipeline(gen_qkv(), 2)
            rot = 48
            if stage == 18:
                kb.barrier()
                dump(qT[:, 0, 0:1024], 1024, b_qT)
                dump(kT[:, 1, 1024:2048], 1024, b_kT)
                dump(v_tok[:, 0:2, :], 1024, b_vt)
                dump_done()
                return nc
            def it_sc(c, t4, rot, wt, bw):
                i2 = rot % 2
                bs = (0, 1, 2) if i2 == 0 else (3, 4, 5)
                tsl = slice(t4 * 512, (t4 + 1) * 512)
                for g3 in range(3):
                    for kc in range(8):
                        mm(PS[:, bs[g3], :], wt[:, kc, g3 * 128:(g3 + 1) * 128], hT[:, kc, tsl], kc == 0, kc == 7,
                           [bw] + b_hT[4 * t4:4 * t4 + 4], [pb[bs[g3]]])
                yield
                cp("act", xss[i2][:], PS[:, bs[2], :], [pb[bs[2]]], [b_xss[i2]])
                yield
                tt("dve", xss[i2][:], PS[:, bs[1], :], xss[i2][:], ALU.mult, [pb[bs[1]], b_xss[i2]], [b_xss[i2]])
                yield
                for _ in conv_g(xss[i2][:], ycv[i2], b_ycv[i2], cws, cwsp, c, [b_xss[i2]], "dve"):
                    yield
                tt("dve", ysc[:, c, tsl], PS[:, bs[0], :], ycv[i2][:], ALU.mult, [pb[bs[0]], b_ycv[i2]],
                   b_ysc[4 * t4:4 * t4 + 4])
                yield

            def gen_sc():
                rot = 0
                getw = prefetched([(lambda c=c: wload(None, parts=[(w_in[:, SC0 + c * 128:SC0 + (c + 1) * 128], 128),
                                                                   (w_in[:, CG0 + c * 128:CG0 + (c + 1) * 128], 128),
                                                                   (w_in[:, XS0 + c * 128:XS0 + (c + 1) * 128], 128)]))
                                   for c in range(4)])
                for c in range(4):
                    wt, bw = getw(c)
                    for t4 in range(NTT):
                        yield it_sc(c, t4, rot, wt, bw)
                        rot += 1
            pipeline(gen_sc(), 2)

            if stage == 21:
                kb.barrier()
                dump(ysc[:, 2, 512:1536], 1024, b_ysc)
                dump_done()
                return nc
            if stage == 2:
                kb.barrier()
                dump(qT[:, 0, 0:1024], 1024, b_qT)
                dump(kT[:, 1, 1024:2048], 1024, b_kT)
                dump(v_tok[:, 0:2, :], 1024, b_vt)
                dump(zs[:, 14:16, :], 1024, b_zs)
                dump(ysc[:, 2, 512:1536], 1024, b_ysc)
                dump(bet[:], 256, [b_bet])
                dump(gg[:], 256, [b_gg])
                dump(egc[:], 256, [b_egc])
                dump(ekt[:], 256, [b_ekt])
                dump(eglp[:].rearrange("p n d q -> p n (d q)"), 128, [b_eglp])
                dump_done()
                return nc
            kb.barrier()
            kb.flush()

        onT = qT; b_onT = b_qT
        if True:
            R2 = Region(D0, AR_BYTES)
            T2 = lambda n, shape, dt=F32: R2.take(shape, dt)
            oacc = T2("oacc", [128, NT, 512]); b_oacc = [kb.bufs(f"oacc{n}_", 2) for n in range(NT)]
            ident4 = T2("ident4", [128, 4, 128], BF16); b_id4 = kb.buf("ident4")
            cp("pool", ident4, identb.unsqueeze(1).to_broadcast([128, 4, 128]), [b_cmb], [b_id4])
            S = []
            for s in range(2):
                d_ = dict(
                    rg=T2(f"rg{s}", [128, 4, 128], BF16), Em=T2(f"Em{s}", [128, 4, 128]),
                    attnT=T2(f"attnT{s}", [128, 4, 128], BF16),
                    VP=[T2(f"VP{s}_{i}", [128, 4, 2, 128], BF16) for i in range(2)],
                    W=[T2(f"W{s}_{i}", [128, 4, 128], BF16) for i in range(2)],
                    PT=[T2(f"PT{s}_{i}", [128, 4, 128], BF16) for i in range(2)],
                    OFF=T2(f"OFF{s}", [128, 4, 3, 128], BF16),
                    T2T=T2(f"T2T{s}", [128, 4, 128], BF16), kqbd=T2(f"kqbd{s}", [128, 8, 128], BF16),
                    ksb=T2(f"ksb{s}", [128, 4, 64], BF16), kg=T2(f"kg{s}", [128, 4, 64], BF16),
                    kt=T2(f"kt{s}", [128, 4, 64], BF16), wT=T2(f"wT{s}", [128, 2, 128], BF16),
                    usb=T2(f"usb{s}", [128, 4, 64]), tmp=T2(f"tmp{s}", [128, 4, 64]),
                    vnb=T2(f"vnb{s}", [128, 4, 64], BF16),
                )
                d_["o1"] = d_["tmp"]
                eoff = R2.pos
                d_["E"] = T2(f"E{s}", [128, 4, 128])
                d_["Y"] = Region(eoff, eoff + 2048).take([128, 2, 4, 128], BF16)
                for nm in ["rg", "E", "Em", "attnT", "OFF", "T2T", "kqbd", "ksb", "kg", "kt", "wT", "usb", "tmp", "vnb"]:
                    d_["b_" + nm] = kb.buf(f"{nm}{s}")
                d_["b_o1"] = d_["b_tmp"]
                d_["b_V"] = kb.bufs(f"V{s}_", 2)
                d_["b_P"] = kb.bufs(f"P{s}_", 2)
                d_["b_W"] = kb.bufs(f"W{s}_", 2)
                d_["b_PT"] = kb.bufs(f"PT{s}_", 2)
                S.append(d_)
            Sm = [[T2(f"Sm{d}{hf}", [128, 2, 64]) for hf in range(2)] for d in range(2)]
            Sbd = [[T2(f"Sbd{d}{hf}", [128, 2, 128], BF16) for hf in range(2)] for d in range(2)]
            inis = [[T2(f"inis{d}{hf}", [128, 2, 64]) for hf in range(2)] for d in range(2)]
            b_Sm = [[kb.buf(f"Sm{d}{hf}") for hf in range(2)] for d in range(2)]
            b_Sbd = [[kb.buf(f"Sbd{d}{hf}") for hf in range(2)] for d in range(2)]
            b_inis = [[kb.buf(f"inis{d}{hf}") for hf in range(2)] for d in range(2)]
            b_sout = [[kb.buf(f"sout{d}{hf}") for hf in range(2)] for d in range(2)]
            op("pool", lambda e: e.memset(oacc, 0.0), [], [b for bb in b_oacc for b in bb])
            for s in range(2):
                op("pool", lambda e, s=s: e.memset(S[s]["kqbd"], 0.0), [], [S[s]["b_kqbd"]])
            for d in range(2):
                for hf in range(2):
                    op("pool", lambda e, d=d, hf=hf: e.memset(Sm[d][hf], 0.0), [], [b_Sm[d][hf]])
                    op("pool", lambda e, d=d, hf=hf: e.memset(Sbd[d][hf], 0.0), [], [b_Sbd[d][hf]])

            def bc(ap, shape, axis):
                return ap.unsqueeze(axis).to_broadcast(shape)

            def problem(d, n, hf):
                s = hf
                Z = S[s]
                base = 4 * s
                B0, B1, B2, B3 = [PS[:, base + i, :] for i in range(4)]
                p0, p1, p2b, p3 = [pb[base + i] for i in range(4)]
                tl = slice(n * 128, (n + 1) * 128)
                hs = slice(d * 8 + 4 * hf, d * 8 + 4 * hf + 4)
                M_rhs = cm_b[:, MFI if d == 0 else MBI, :]
                M_lhs = cm_b[:, MBS if d == 0 else MFS, :]
                M_inc = cm_b[:, MFI if d == 0 else MBI, :]
                M_str = cm_b[:, MFS if d == 0 else MBS, :]
                f3 = [128, 4, 128]
                v3 = lambda bank_ap: bank_ap.rearrange("p (h c) -> p h c", h=4)
                VP, W, PT, OFF, Y = Z["VP"], Z["W"], Z["PT"], Z["OFF"], Z["Y"]
                bV, bP, bW, bPT, bOFF, bE = Z["b_V"], Z["b_P"], Z["b_W"], Z["b_PT"], Z["b_OFF"], Z["b_E"]
                tt("dve", Z["rg"], bc(gg[:, n, hs], f3, 2), bc(M_rhs, f3, 1), ALU.mult, [b_gg, b_cmb], [Z["b_rg"]])
                if stage == 60:
                    yield "stop"
                mm(B2, M_lhs, Z["rg"].rearrange("p h c -> p (h c)"), True, False, [b_cmb, Z["b_rg"]], [p2b], inc=False)
                mm(B2, cm_b[:, NEGTF if d == 0 else NEGTB, :], ident4.rearrange("p h c -> p (h c)"), False, True,
                   [b_cmb, b_id4], [p2b])
                kq5 = Z["kqbd"].rearrange("p (q t hh) c -> p q t hh c", q=2, t=2)
                cp("act", kq5[0:64, :, 0, 0, :], kT[0:64, 2 * hf:2 * hf + 2, tl], [b_kT[n]], [Z["b_kqbd"]])
                cp("act", kq5[64:128, :, 0, 1, :], kT[64:128, 2 * hf:2 * hf + 2, tl], [b_kT[n]], [Z["b_kqbd"]])
                cp("act", kq5[0:64, :, 1, 0, :], qT[0:64, 2 * hf:2 * hf + 2, tl], [b_qT[n]], [Z["b_kqbd"]])
                cp("act", kq5[64:128, :, 1, 1, :], qT[64:128, 2 * hf:2 * hf + 2, tl], [b_qT[n]], [Z["b_kqbd"]])
                for q2 in range(2):
                    mm(PS[:, base + q2, :], kT[:, 2 * hf + q2, tl],
                       Z["kqbd"][:, 4 * q2:4 * q2 + 4, :].rearrange("p a c -> p (a c)"), True, True,
                       [b_kT[n], Z["b_kqbd"]], [pb[base + q2]])
                act(Z["Em"], v3(B2), AF.Exp, [p2b], [Z["b_Em"]])
                if ((n % 2 == 0) if d == 0 else (n % 2 == 1)):
                    sm_, sbd_, bsm_, bsbd_ = Sm[d][hf], Sbd[d][hf], b_Sm[d][hf], b_Sbd[d][hf]
                    kb.dma("sp", inis[d][hf][:], init_s[d, n // 2, 2 * hf:2 * hf + 2].rearrange("q p e -> p q e"),
                           writes=[b_inis[d][hf]])
                    stt("dve", sm_[:], sm_[:], flg[:, 0:1], inis[d][hf][:], ALU.mult, ALU.add, [bsm_, b_flg, b_inis[d][hf]], [bsm_])
                    cp("act", sbd_[0:64, :, 0:64], sm_[0:64, :, :], [bsm_], [bsbd_])
                    cp("act", sbd_[64:128, :, 64:128], sm_[64:128, :, :], [bsm_], [bsbd_])
                for q2 in range(2):
                    mm(B3[:, q2 * 128:(q2 + 1) * 128], kT[:, 2 * hf + q2, tl], identb, True, True, [b_kT[n], b_cmb], [p3],
                       inc=(q2 == 1))
                act(Z["ksb"][:], B3[:, 0:256].rearrange("p (h c) -> p h c", h=4), AF.Copy, [p3], [Z["b_ksb"]])
                tt("pool", Z["kg"][:], Z["ksb"][:], bc(egc[:, n, hs], [128, 4, 64], 2), ALU.mult, [Z["b_ksb"], b_egc], [Z["b_kg"]])
                tt("pool", Z["kt"][:], Z["ksb"][:], bc(ekt[:, n, hs], [128, 4, 64], 2), ALU.mult, [Z["b_ksb"], b_ekt], [Z["b_kt"]])
                yield
                s4 = lambda ap: ap.rearrange("p (a b) c -> p a b c", a=2)
                kk4 = PS[:, base:base + 2, 0:256].rearrange("p q (hh c) -> p q hh c", hh=2)
                kq4 = PS[:, base:base + 2, 256:512].rearrange("p q (hh c) -> p q hh c", hh=2)
                V0f = VP[1][:, :, 0, :]
                W0f = W[1]
                for h in range(4):
                    kk_h = PS[:, base + h // 2, (h % 2) * 128:(h % 2) * 128 + 128]
                    stt("dve", V0f[:, h, :], kk_h, nbet[:, n, d * 8 + 4 * hf + h:d * 8 + 4 * hf + h + 1], Z["Em"][:, h, :],
                        ALU.mult, ALU.mult, [pb[base + h // 2], b_nbet, Z["b_Em"]], [bV[1]])
                for h in range(4):
                    mm(B2[:, h * 128:(h + 1) * 128], V0f[:, h, :], identb, True, True, [bV[1], b_cmb], [p2b], inc=(h == 3))
                act(W0f, v3(B2), AF.Copy, [p2b], [bW[1]])
                tt("dve", s4(Z["attnT"]), kq4, s4(Z["Em"]), ALU.mult, [p0, p1, Z["b_Em"]], [Z["b_attnT"]])
                yield
                mBD, mM1, mM2 = [bc(cm_b[:, i, :], f3, 1) for i in (BD32, M1I, M2I)]
                tt("pool", VP[0][:, :, 0, :], V0f, mBD, ALU.mult, [bV[1], b_cmb], [bV[0]])
                tt("dve", OFF[:, :, 0, :], V0f, mM1, ALU.mult, [bV[1], b_cmb], [bOFF])
                tt("dve", W[0], W0f, mBD, ALU.mult, [bW[1], b_cmb], [bW[0]])
                tt("dve", OFF[:, :, 1, :], W0f, mM1, ALU.mult, [bW[1], b_cmb], [bOFF])
                tt("pool", OFF[:, :, 2, :], W0f, mM2, ALU.mult, [bW[1], b_cmb], [bOFF])
                tt("pool", VP[1][:, :, 1, :], VP[0][:, :, 0, :], bc(identb, f3, 1), ALU.add, [bV[0], b_cmb], [bP[1]])
                tt("pool", PT[1], W[0], bc(identb, f3, 1), ALU.add, [bW[0], b_cmb], [bPT[1]])
                yield
                idf3 = bc(idf[:], f3, 1)
                for k in range(5):
                    cur, nxt = k % 2, 1 - (k % 2)
                    need_sq = k <= 3
                    need_q = k >= 1
                    for h in range(4):
                        hsl = slice(h * 128, (h + 1) * 128)
                        if need_sq:
                            mm(B0[:, hsl], W[cur][:, h, :], VP[cur][:, h, 0, :], True, True, [bW[cur], bV[cur]], [p0], inc=(h == 3))
                    for h in range(4):
                        hsl = slice(h * 128, (h + 1) * 128)
                        if need_sq:
                            mm(B2[:, hsl], VP[cur][:, h, 0, :], W[cur][:, h, :], True, True, [bW[cur], bV[cur]], [p2b], inc=(h == 3))
                    for h in range(4):
                        hsl = slice(h * 128, (h + 1) * 128)
                        if need_q:
                            mm(B1[:, hsl], W[cur][:, h, :], VP[cur][:, h, 1, :], True, False, [bW[cur], bP[cur]], [p1], inc=False)
                            mm(B1[:, hsl], identb, VP[cur][:, h, 1, :], False, True, [b_cmb, bP[cur]], [p1], inc=(h == 3))
                    for h in range(4):
                        hsl = slice(h * 128, (h + 1) * 128)
                        if need_q:
                            mm(B3[:, hsl], VP[cur][:, h, 1, :], W[cur][:, h, :], True, False, [bW[cur], bP[cur]], [p3], inc=False)
                            mm(B3[:, hsl], VP[cur][:, h, 1, :], identb, False, True, [b_cmb, bP[cur]], [p3], inc=(h == 3))
                    if need_sq:
                        act(VP[nxt][:, :, 0, :], v3(B0), AF.Copy, [p0], [bV[nxt]])
                    if need_q:
                        cp("dve", VP[nxt][:, :, 1, :], v3(B1), [p1], [bP[nxt]])
                    if need_sq:
                        act(W[nxt], v3(B2), AF.Copy, [p2b], [bW[nxt]])
                    if need_q:
                        cp("act" if k % 2 == 1 else "dve", PT[nxt], v3(B3), [p3], [bPT[nxt]])
                    yield
                Xd, XdT = VP[1][:, :, 1, :], PT[1]
                for h in range(4):
                    mm(B0[:, h * 128:(h + 1) * 128], OFF[:, h, 1, :], Xd[:, h, :], True, True, [bOFF, bP[1]], [p0], inc=(h == 3))
                for h in range(4):
                    mm(B1[:, h * 128:(h + 1) * 128], OFF[:, h, 0, :], XdT[:, h, :], True, True, [bOFF, bPT[1]], [p1], inc=(h == 3))
                act(Y[:, 0, :, :], v3(B0), AF.Copy, [p0], [bE])
                cp("dve", Y[:, 1, :, :], v3(B1), [p1], [bE])
                yield
                for h in range(4):
                    hsl = slice(h * 128, (h + 1) * 128)
                    mm(B2[:, hsl], XdT[:, h, :], Y[:, 0, h, :], True, False, [bPT[1], bE], [p2b], inc=False)
                    mm(B2[:, hsl], XdT[:, h, :], identb, False, True, [bPT[1], b_cmb], [p2b], inc=(h == 3))
                for h in range(4):
                    hsl = slice(h * 128, (h + 1) * 128)
                    mm(B3[:, hsl], Xd[:, h, :], Y[:, 1, h, :], True, False, [bP[1], bE], [p3], inc=False)
                    mm(B3[:, hsl], Xd[:, h, :], identb, False, True, [bP[1], b_cmb], [p3], inc=(h == 3))
                X64, X64T = VP[0][:, :, 1, :], PT[0]
                act(X64, v3(B2), AF.Copy, [p2b], [bP[0]])
                cp("dve", X64T, v3(B3), [p3], [bPT[0]])
                yield
                for h in range(4):
                    mm(B0[:, h * 128:(h + 1) * 128], OFF[:, h, 2, :], X64[:, h, :], True, True, [bOFF, bP[0]], [p0], inc=(h == 3))
                act(Y[:, 0, :, :], v3(B0), AF.Copy, [p0], [bE])
                for h in range(4):
                    hsl = slice(h * 128, (h + 1) * 128)
                    mm(B1[:, hsl], X64T[:, h, :], Y[:, 0, h, :], True, False, [bPT[0], bE], [p1], inc=False)
                    mm(B1[:, hsl], X64T[:, h, :], identb, False, True, [bPT[0], b_cmb], [p1], inc=(h == 3))
                cp("dve", Z["T2T"], v3(B1), [p1], [Z["b_T2T"]])
                yield
                f64 = [128, 4, 64]
                for h in range(4):
                    hd = 4 * hf + h
                    mm(B3[:, 256 + h * 64:256 + (h + 1) * 64], Z["T2T"][:, h, :], v_tok[:, n, hd * 64:(hd + 1) * 64], True, True,
                       [Z["b_T2T"], b_vt[n]], [p3], inc=(h == 3))
                for q2 in range(2):
                    for hh in range(2):
                        mm(B0[:, (q2 * 2 + hh) * 128:(q2 * 2 + hh + 1) * 128],
                           Z["kg"][:, 2 * q2:2 * q2 + 2, :].rearrange("p a c -> p (a c)"), Z["T2T"][:, 2 * q2 + hh, :], True, True,
                           [Z["b_kg"], Z["b_T2T"]], [p0], inc=(q2 == 1 and hh == 1))
                act(Z["usb"][:], B3[:, 256:512].rearrange("p (h c) -> p h c", h=4), AF.Copy, [p3], [Z["b_usb"]])
                b04 = B0.rearrange("p (q a c) -> p q a c", q=2, a=2)
                act(Z["wT"][0:64, :, :], b04[0:64, :, 0, :], AF.Copy, [p0], [Z["b_wT"]])
                act(Z["wT"][64:128, :, :], b04[64:128, :, 1, :], AF.Copy, [p0], [Z["b_wT"]])
                yield
                sm, sbd, bsm, bsbd = Sm[d][hf], Sbd[d][hf], b_Sm[d][hf], b_Sbd[d][hf]
                first = (n % 2 == 0) if d == 0 else (n % 2 == 1)
                last = not first
                for q2 in range(2):
                    mm(B1[:, q2 * 128:(q2 + 1) * 128], Z["wT"][:, q2, :], sbd[:, q2, :], True, True, [Z["b_wT"], bsbd], [p1],
                       inc=False)
                for q2 in range(2):
                    mm(B1[:, 256 + q2 * 128:256 + (q2 + 1) * 128], qT[:, 2 * hf + q2, tl], sbd[:, q2, :], True, True,
                       [b_qT[n], bsbd], [p1], inc=(q2 == 1))
                tt("dve", Z["tmp"][:], B1[:, 0:256].rearrange("p (h c) -> p h c", h=4), Z["usb"][:], ALU.subtract,
                   [p1, Z["b_usb"]], [Z["b_tmp"]])
                tt("dve", Z["vnb"][:], Z["tmp"][:], bc(nbet[:, n, hs], f64, 2), ALU.mult, [Z["b_tmp"], b_nbet], [Z["b_vnb"]])
                for h in range(4):
                    mm(B2[:, h * 64:(h + 1) * 64], Z["attnT"][:, h, :], Z["vnb"][:, h, :], True, True,
                       [Z["b_attnT"], Z["b_vnb"]], [p2b], inc=False)
                for q2 in range(2):
                    mm(B2[:, 256 + q2 * 128:256 + (q2 + 1) * 128], Z["kt"][:, 2 * q2:2 * q2 + 2, :].rearrange("p a c -> p (a c)"),
                       Z["vnb"][:, 2 * q2:2 * q2 + 2, :].rearrange("p a c -> p (a c)"), True, True,
                       [Z["b_kt"], Z["b_vnb"]], [p2b], inc=(q2 == 1))
                yield
                ps4 = B2[:, 256:512].rearrange("p (q c) -> p q c", q=2)
                tt("dve", sm[:], sm[:], bc(eglp[:, n, d, 2 * hf:2 * hf + 2], [128, 2, 64], 2), ALU.mult, [bsm, b_eglp], [bsm])
                tt("dve", sm[0:64, :, :], sm[0:64, :, :], ps4[0:64, :, 0:64], ALU.add, [bsm, p2b], [bsm])
                tt("dve", sm[64:128, :, :], sm[64:128, :, :], ps4[64:128, :, 64:128], ALU.add, [bsm, p2b], [bsm])
                cp("act", sbd[0:64, :, 0:64], sm[0:64, :, :], [bsm], [bsbd])
                cp("act", sbd[64:128, :, 64:128], sm[64:128, :, :], [bsm], [bsbd])
                tt("dve", Z["o1"][:], B1[:, 256:512].rearrange("p (h c) -> p h c", h=4), bc(egc[:, n, hs], f64, 2), ALU.mult,
                   [p1, b_egc], [Z["b_o1"]])
                tt("dve", Z["o1"][:], B2[:, 0:256].rearrange("p (h c) -> p h c", h=4), Z["o1"][:], ALU.add,
                   [p2b, Z["b_o1"]], [Z["b_o1"]])
                osl = oacc[:, n, 256 * hf:256 * hf + 256].rearrange("p (h c) -> p h c", h=4)
                tt("pool", osl, osl, Z["o1"][:], ALU.add, [Z["b_o1"], b_oacc[n][hf]], [b_oacc[n][hf]])
                if last:
                    kb.dma("sp", s_out[d, n // 2, 2 * hf:2 * hf + 2].rearrange("q p e -> p q e"), sm[:], reads=[bsm], track=b_sout[d][hf])
                yield

            ycount = [0]

            def run_pair(g0, g1):
                gens = [g0, g1]
                while gens:
                    for g in list(gens):
                        try:
                            if next(g) == "stop":
                                return True
                            ycount[0] += 1
                            if 30 <= stage < 60 and ycount[0] >= 2 * (stage - 30 + 1):
                                return True
                        except StopIteration:
                            gens.remove(g)
                return False

            nsteps = NT if stage != 3 else 2
            for st in range(nsteps):
                for d in range(2):
                    n = st if d == 0 else NT - 1 - st
                    if run_pair(problem(d, n, 0), problem(d, n, 1)):
                        break
                if 30 <= stage < 70:
                    break

            if stage in (3, 4) or 30 <= stage < 70:
                kb.barrier()
                dump(oacc[:, 0, :], 512, b_oacc[0])
                dump(oacc[:, 1, :], 512, b_oacc[1])
                dump(oacc[:, 15, :], 512, b_oacc[15])
                dump(oacc[:, 7, :], 512, b_oacc[7])
                dump(Sm[0][0][:], 128, [b_Sm[0][0]])
                dump(Sm[1][1][:], 128, [b_Sm[1][1]])
                dump(S[0]["T2T"][:], 512, [S[0]["b_T2T"]])
                dump(S[0]["attnT"][:], 512, [S[0]["b_attnT"]])
                dump(S[0]["usb"][:], 256, [S[0]["b_usb"]])
                dump(S[0]["wT"][:], 256, [S[0]["b_wT"]])
                dump_done()
                return nc
            kb.barrier()

            onr = T("onr", [128, 64]); b_onr = kb.buf("onr")
            kb.dma("sp", onr[:], onorm.partition_broadcast(128), writes=[b_onr])
            g_sq = [S[i]["Em"].rearrange("p h c -> p (h c)") for i in range(2)]
            g_on = [S[i]["E"].rearrange("p h c -> p (h c)") for i in range(2)]
            g_ob = [S[i]["OFF"].rearrange("p a b c -> p (a b c)")[:, 0:512] for i in range(2)]
            g_st = [S[i]["usb"].rearrange("p h c -> p (h c)") for i in range(2)]
            bg_sq, bg_on, bg_ob, bg_st = [kb.bufs(nm, 2) for nm in ("gsq", "gon", "gob", "gst")]
            def it_gate(n):
                i2 = n % 2
                o3 = oacc[:, n, :].rearrange("p (h c) -> p h c", h=8)
                sq3 = g_sq[i2].rearrange("p (h c) -> p h c", h=8)
                on3 = g_on[i2].rearrange("p (h c) -> p h c", h=8)
                rdo = b_oacc[n]
                tt("pool", g_sq[i2], oacc[:, n, :], oacc[:, n, :], ALU.mult, rdo, [bg_sq[i2]])
                yield
                op("dve", lambda e, i2=i2, sq3=sq3: e.tensor_reduce(out=g_st[i2][:, 0:8], in_=sq3, axis=AX.X, op=ALU.add),
                   [bg_sq[i2]], [bg_st[i2]])
                yield
                act(g_st[i2][:, 8:16], g_st[i2][:, 0:8], AF.Sqrt, [bg_st[i2], b_eps], [bg_st[i2]], scale=1.0 / 64, bias=eps_t[:, 0:1])
                yield
                op("dve", lambda e, i2=i2: e.reciprocal(out=g_st[i2][:, 16:24], in_=g_st[i2][:, 8:16]), [bg_st[i2]], [bg_st[i2]])
                tt("dve", on3, o3, bc(g_st[i2][:, 16:24], [128, 8, 64], 2), ALU.mult, rdo + [bg_st[i2]], [bg_on[i2]])
                yield
                tt("pool", on3, on3, bc(onr[:], [128, 8, 64], 1), ALU.mult, [bg_on[i2], b_onr], [bg_on[i2]])
                tt("pool", g_ob[i2], g_on[i2], zs[:, n, :], ALU.mult, [bg_on[i2], b_zs[n]], [bg_ob[i2]])
                yield
                bk = i2
                for c in range(4):
                    mm(PS[:, bk, c * 128:(c + 1) * 128], g_ob[i2][:, c * 128:(c + 1) * 128], identb, True, True,
                       [bg_ob[i2], b_cmb], [pb[bk]], inc=(c == 3))
                yield
                cp("act", onT[:, :, n * 128:(n + 1) * 128], PS[:, bk, :].rearrange("p (c t) -> p c t", c=4), [pb[bk]], [b_onT[n]])
                yield
            pipeline((it_gate(n) for n in range(NT)), 2)

            if stage == 5:
                kb.barrier()
                dump(onT[:, 0, 0:1024], 1024, b_onT)
                dump(onT[:, 3, 1024:2048], 1024, b_onT)
                dump_done()
                return nc
            kb.barrier()
            kb.flush()

        if True:
            x1 = Region(56 * KB_, 88 * KB_).take([128, 8, DM])
            h2T = Region(88 * KB_, 104 * KB_).take([128, 8, 1024], BF16)
            y2a = Region(88 * KB_, 104 * KB_).take([128, 8, 512])
            R3 = Region(D0, AR_BYTES)
            WBH['wb'] = [R3.take([128, 8, 512], BF16) for i in range(NWB)]
            wb_ctr[0] = 0
            b_wb3 = kb.bufs("wb3_", 3)
            for i in range(3):
                b_wb[i] = b_wb3[i]
            D3 = R3.pos
            hTh = Region(D3, D3 + 16 * KB_).take([128, 8, 1024], BF16)
            mT = Region(D3 + 16 * KB_, D3 + 32 * KB_).take([128, 8, 1024], BF16)
            actT = Region(D3, D3 + 44 * KB_).take([128, 22, 1024], BF16)
            R3s = Region(D3 + 44 * KB_, AR_BYTES)
            t1p2 = [R3s.take([128, DM]) for i in range(2)]
            t1p = t1p2[0]
            htk = [R3s.take([128, DM], BF16) for i in range(2)]
            sga = [R3s.take([128, 512]) for i in range(2)]
            sgb = [R3s.take([128, 512]) for i in range(2)]
            st3 = R3s.take([128, 64])
            b_x1 = kb.bufs("x1_", 8); b_h2T = kb.bufs("h2T_", 8); b_y2a = kb.bufs("y2a_", 8)
            b_hTh = kb.bufs("hTh_", 8); b_mT = kb.bufs("mT_", 8); b_actT = kb.bufs("actT_", 2)
            b_t1p2 = kb.bufs("t1p", 2); b_t1p = b_t1p2[0]; b_htk = kb.bufs("htk", 2); b_sga = kb.bufs("sga", 2); b_sgb = kb.bufs("sgb", 2)
            b_st3 = kb.bufs("st3_", 64)
            b_yout = kb.bufs("yout", 8)

            def norm_transpose(src_tile, rd, Ar, Br, bA, bB, dstT, bdst, j, scol, pre=None):
                i2 = j % 2
                c0, c1, c2 = scol, scol + 1, scol + 2
                tp_, btp = t1p2[i2], b_t1p2[i2]
                if pre is not None:
                    pre()
                act(htk[i2], src_tile, AF.Square, rd, [b_htk[i2], b_st3[c0]], accum_out=st3[:, c0:c0 + 1])
                act(st3[:, c1:c1 + 1], st3[:, c0:c0 + 1], AF.Sqrt, [b_st3[c0], b_eps], [b_st3[c1]], scale=1.0 / DM, bias=eps_t[:, 0:1])
                yield
                op("dve", lambda e: e.reciprocal(out=st3[:, c2:c2 + 1], in_=st3[:, c1:c1 + 1]), [b_st3[c1]], [b_st3[c2]])
                stt("dve", tp_, src_tile, st3[:, c2:c2 + 1], Ar, ALU.mult, ALU.mult, rd + [b_st3[c2], bA], [btp])
                yield
                tt("dve", htk[i2], tp_, Br, ALU.add, [btp, bB], [b_htk[i2]])
                yield
                bk = 2 * i2
                for kc in range(8):
                    mm(PS[:, bk + kc // 4, (kc % 4) * 128:(kc % 4 + 1) * 128], htk[i2][:, kc * 128:(kc + 1) * 128], identb,
                       True, True, [b_htk[i2], b_cmb], [pb[bk + kc // 4]], inc=(kc % 4 == 3))
                yield
                cp("act", dstT[:, :, j * 128:(j + 1) * 128],
                   PS[:, bk:bk + 2, :].rearrange("p a (c t) -> p (a c) t", t=128), [pb[bk], pb[bk + 1]], [bdst])
                yield

            for half in range(2):
                T0_ = half * 1024
                def ld(j):
                    return lambda: kb.dma("sp", x1[:, j, :], x[T0_ + j * 128:T0_ + (j + 1) * 128, :], writes=[b_x1[j]])
                for j in range(8):
                    ld(j)()
                pipeline((norm_transpose(x1[:, j, :], [b_x1[j]], A1r, B1r, b_modr[1], b_modr[0], hTh, b_hTh[j], j, 3 * j)
                          for j in range(8)), 2)
                if stage == 70:
                    kb.barrier()
                    dump(hTh[:, 0, :], 1024, b_hTh)
                    dump_done()
                    return nc
                def it_a2(c, t2, rot, wt, bw):
                    i2 = rot % 2
                    bs = (0, 1, 2, 3) if i2 == 0 else (4, 5, 6, 7)
                    hsl = slice(t2 * 512, (t2 + 1) * 512)
                    gsl = slice(T0_ + t2 * 512, T0_ + (t2 + 1) * 512)
                    rdh = b_hTh[4 * t2:4 * t2 + 4]
                    gt0 = (T0_ + t2 * 512) // 128
                    for kc in range(8):
                        mm(PS[:, bs[0], :], wt[:, kc, 0:128], hTh[:, kc, hsl], kc == 0, kc == 7, [bw] + rdh, [pb[bs[0]]])
                    for kc in range(8):
                        mm(PS[:, bs[1], :], wt[:, kc, 128:256], hTh[:, kc, hsl], kc == 0, kc == 7, [bw] + rdh, [pb[bs[1]]])
                    for kc in range(4):
                        mm(PS[:, bs[2], :], wt[:, kc, 256:384], onT[:, kc, gsl], kc == 0, kc == 3, [bw] + b_onT[gt0:gt0 + 4], [pb[bs[2]]])
                    for kc in range(4):
                        mm(PS[:, bs[3], :], wt[:, kc, 384:512], ysc[:, kc, gsl], kc == 0, kc == 3, [bw] + b_ysc[gt0:gt0 + 4], [pb[bs[3]]])
                    yield
                    act(sga[i2], PS[:, bs[0], :], AF.Sigmoid, [pb[bs[0]]], [b_sga[i2]])
                    act(sgb[i2], PS[:, bs[1], :], AF.Sigmoid, [pb[bs[1]]], [b_sgb[i2]])
                    yield
                    tt("dve", sga[i2], PS[:, bs[2], :], sga[i2], ALU.mult, [pb[bs[2]], b_sga[i2]], [b_sga[i2]])
                    tt("dve", sgb[i2], PS[:, bs[3], :], sgb[i2], ALU.mult, [pb[bs[3]], b_sgb[i2]], [b_sgb[i2]])
                    yield
                    tt("pool", mT[:, c, hsl], sga[i2], sgb[i2], ALU.add, [b_sga[i2], b_sgb[i2]], b_mT[4 * t2:4 * t2 + 4])
                    yield

                def gen_a2():
                    rot = 0

                    def ld_a2(c):
                        csl = slice(c * 128, (c + 1) * 128)
                        wt, bw = wload(None, parts=[(w_in[:, GA0 + c * 128:GA0 + (c + 1) * 128], 128),
                                                    (w_in[:, GB0 + c * 128:GB0 + (c + 1) * 128], 128)])
                        kb.dma("pool", wt[:, 0:4, 256:384], w_br_a[:, csl].rearrange("(kc p) c -> p kc c", p=128), writes=[bw], partial=True)
                        kb.dma("pool", wt[:, 0:4, 384:512], w_br_b[:, csl].rearrange("(kc p) c -> p kc c", p=128), writes=[bw], partial=True)
                        return wt, bw
                    getw = prefetched([(lambda c=c: ld_a2(c)) for c in range(8)])
                    for c in range(8):
                        wt, bw = getw(c)
                        for t2 in range(2):
                            yield it_a2(c, t2, rot, wt, bw)
                            rot += 1
                pipeline(gen_a2(), 2)
                if stage == 71:
                    kb.barrier()
                    dump(mT[:, 0, :], 1024, b_mT)
                    dump_done()
                    return nc
                wo = []
                for nb in range(2):
                    wo.append(wload(w_out[:, nb * 512:(nb + 1) * 512]))
                def it_a3(j):
                    i2 = j % 2
                    bk = 2 * i2 + 4
                    tp_, btp = t1p2[i2], b_t1p2[i2]
                    for nb in range(2):
                        for kc in range(8):
                            mm(PS[:, bk + nb, :], mT[:, kc, j * 128:(j + 1) * 128], wo[nb][0][:, kc, :], kc == 0, kc == 7,
                               [b_mT[j], wo[nb][1]], [pb[bk + nb]])
                    yield
                    ps2 = PS[:, bk:bk + 2, :].rearrange("p a b -> p (a b)")
                    rd = [pb[bk], pb[bk + 1]]
                    c0 = 24 + 3 * j
                    act(htk[i2], ps2, AF.Square, rd, [b_htk[i2], b_st3[c0]], accum_out=st3[:, c0:c0 + 1])
                    act(st3[:, c0 + 1:c0 + 2], st3[:, c0:c0 + 1], AF.Sqrt, [b_st3[c0], b_eps], [b_st3[c0 + 1]], scale=1.0 / DM, bias=eps_t[:, 0:1])
                    yield
                    op("dve", lambda e, c0=c0: e.reciprocal(out=st3[:, c0 + 2:c0 + 3], in_=st3[:, c0 + 1:c0 + 2]), [b_st3[c0 + 1]], [b_st3[c0 + 2]])
                    stt("dve", tp_, ps2, st3[:, c0 + 2:c0 + 3], C1r, ALU.mult, ALU.mult, rd + [b_st3[c0 + 2], b_modr[2]], [btp])
                    yield
                    tt("dve", x1[:, j, :], x1[:, j, :], tp_, ALU.add, [b_x1[j], btp], [b_x1[j]])
                    yield
                pipeline((it_a3(j) for j in range(8)), 2)
                pipeline((norm_transpose(x1[:, j, :], [b_x1[j]], A2r, B2r, b_modr[4], b_modr[3], h2T, b_h2T[j], j, 3 * j)
                          for j in range(8)), 2)
                kb.barrier()
                if stage == 73:
                    kb.barrier()
                    dump(h2T[:, 0, :], 1024, b_h2T)
                    dump_done()
                    return nc
                def it_b1(c, cc, t2, rot, wt, bw):
                    i2 = rot % 2
                    bs = (0, 1) if i2 == 0 else (2, 3)
                    hsl = slice(t2 * 512, (t2 + 1) * 512)
                    rdh = b_h2T[4 * t2:4 * t2 + 4]
                    for kc in range(8):
                        mm(PS[:, bs[0], :], wt[:, kc, cc * 128:(cc + 1) * 128], h2T[:, kc, hsl], kc == 0, kc == 7, [bw] + rdh, [pb[bs[0]]])
                    for kc in range(8):
                        mm(PS[:, bs[1], :], wt[:, kc, 256 + cc * 128:256 + (cc + 1) * 128], h2T[:, kc, hsl], kc == 0, kc == 7, [bw] + rdh, [pb[bs[1]]])
                    yield
                    act(sga[i2], PS[:, bs[0], :], AF.Silu, [pb[bs[0]]], [b_sga[i2]])
                    yield
                    tt("dve", actT[:, c, hsl], PS[:, bs[1], :], sga[i2], ALU.mult, [pb[bs[1]], b_sga[i2]], [b_actT[t2]])
                    yield

                def gen_b1():
                    rot = 0
                    getw = prefetched([(lambda c2=c2: wload(None, parts=[(w_ffn_in[:, c2 * 256:(c2 + 1) * 256], 256),
                                                                        (w_ffn_in[:, DFF + c2 * 256:DFF + (c2 + 1) * 256], 256)]))
                                       for c2 in range(11)])
                    for c2 in range(11):
                        wt, bw = getw(c2)
                        for cc in range(2):
                            for t2 in range(2):
                                yield it_b1(2 * c2 + cc, cc, t2, rot, wt, bw)
                                rot += 1
                pipeline(gen_b1(), 2)
                kb.barrier()
                if stage == 74:
                    kb.barrier()
                    dump(actT[:, 0, :], 1024, b_actT)
                    dump_done()
                    return nc
                getw2 = prefetched([(lambda ch=ch, g8=g8: wload(w_ffn_out[g8 * 1024:g8 * 1024 + (8 if g8 < 2 else 6) * 128,
                                                                          ch * 512:(ch + 1) * 512]))
                                    for ch in range(2) for g8 in range(3)])
                for ch in range(2):
                    for g8 in range(3):
                        nk = 8 if g8 < 2 else 6
                        wt, bw = getw2(ch * 3 + g8)
                        for kl in range(nk):
                            kc = g8 * 8 + kl
                            for j in range(8):
                                mm(PS[:, j, :], actT[:, kc, j * 128:(j + 1) * 128], wt[:, kl, :], kc == 0, kc == 21,
                                   [bw, b_actT[j // 4]], [pb[j]], inc=(kc == 21) or (kl == nk - 1 and j == 7))
                    if stage == 75:
                        kb.barrier()
                        dump(PS[:, 0, :], 512, [pb[0]])
                        dump_done()
                        return nc
                    def it_b2e(j, ch=ch):
                        c0 = 48 + 2 * j
                        i2 = j % 2
                        tp_, btp = t1p2[i2], b_t1p2[i2]
                        if ch == 0:
                            act(htk[i2][:, 0:512], PS[:, j, :], AF.Square, [pb[j]], [b_htk[i2], b_st3[c0]], accum_out=st3[:, c0:c0 + 1])
                            cp("act", y2a[:, j, :], PS[:, j, :], [pb[j]], [b_y2a[j]])
                            yield
                        else:
                            act(htk[i2][:, 0:512], PS[:, j, :], AF.Square, [pb[j]], [b_htk[i2], b_st3[c0 + 1]], accum_out=st3[:, c0 + 1:c0 + 2])
                            yield
                            tt("dve", st3[:, c0:c0 + 1], st3[:, c0:c0 + 1], st3[:, c0 + 1:c0 + 2], ALU.add, [b_st3[c0], b_st3[c0 + 1]], [b_st3[c0]])
                            yield
                            act(st3[:, c0 + 1:c0 + 2], st3[:, c0:c0 + 1], AF.Sqrt, [b_st3[c0], b_eps], [b_st3[c0 + 1]], scale=1.0 / DM, bias=eps_t[:, 0:1])
                            yield
                            op("dve", lambda e, c0=c0: e.reciprocal(out=st3[:, c0:c0 + 1], in_=st3[:, c0 + 1:c0 + 2]), [b_st3[c0 + 1]], [b_st3[c0]])
                            stt("dve", tp_[:, 0:512], y2a[:, j, :], st3[:, c0:c0 + 1], C2r[:, 0:512], ALU.mult, ALU.mult,
                                [b_y2a[j], b_st3[c0], b_modr[5]], [btp])
                            stt("dve", tp_[:, 512:1024], PS[:, j, :], st3[:, c0:c0 + 1], C2r[:, 512:1024], ALU.mult, ALU.mult,
                                [pb[j], b_st3[c0], b_modr[5]], [btp])
                            yield
                            tt("dve", x1[:, j, :], x1[:, j, :], tp_, ALU.add, [b_x1[j], btp], [b_x1[j]])
                            yield
                            kb.dma("sp", y[T0_ + j * 128:T0_ + (j + 1) * 128, :], x1[:, j, :], reads=[b_x1[j]], track=b_yout[j])
                            yield
                    pipeline((it_b2e(j) for j in range(8)), 2)
                    if stage == 76:
                        kb.barrier()
                        dump(y2a[:, 0, :], 512, [b_y2a[0]])
                        dump_done()
                        return nc
                kb.barrier()
                if stage == 77:
                    dump(x1[:, 0, :], 1024, [b_x1[0]])
                    dump_done()
                    return nc

        if stage == 6:
            dump(x1[:, 0, :], 1024, [b_x1[0]])
            dump_done()
            return nc
        kb.finish()
    return nc


def _const_mats():
    a = np.arange(128)[:, None]
    b = np.arange(128)[None, :]
    m = np.zeros((NMAT, 128, 128), np.float32)
    m[IDENT] = (a == b)
    m[MFI] = (a <= b)
    m[MFS] = (a < b)
    m[MBI] = (a >= b)
    m[MBS] = (a > b)
    m[BONES] = ((a // 64) == (b // 64))
    m[ONES] = 1.0
    m[BD32] = ((a // 32) == (b // 32)) & (a != b)
    m[NEGTF] = np.where(b <= a, 0.0, -30000.0)
    m[NEGTB] = np.where(b >= a, 0.0, -30000.0)
    m[M1I] = ((a // 64) == (b // 64)) & ((a // 32) != (b // 32))
    m[M2I] = ((a // 64) != (b // 64))
    return m


def make_in_maps(inp):
    f = lambda a: np.ascontiguousarray(np.asarray(a, dtype=np.float32))
    shared = dict(
        w_mod=f(inp["w_mod"][0]), b_mod=f(inp["b_mod"][0]), norm_gains=f(inp["norm_gains"][0]),
        w_in=f(inp["w_in"][0]),
        cwq=f(np.asarray(inp["conv_qkv"][0]).reshape(3, 12, 128).transpose(2, 1, 0)),
        cws=f(np.asarray(inp["conv_sc"][0]).reshape(3, 4, 128).transpose(2, 1, 0)),
        a_log=f(np.asarray(inp["a_log"][0]).reshape(16)), dt_bias=f(np.asarray(inp["dt_bias"][0]).reshape(16)),
        onorm=f(inp["onorm"][0]), w_br_a=f(inp["w_branch_a"][0]), w_br_b=f(inp["w_branch_b"][0]),
        w_out=f(inp["w_out"][0]), w_ffn_in=f(inp["w_ffn_in"][0]), w_ffn_out=f(inp["w_ffn_out"][0]),
        cmat=_const_mats(),
    )
    maps = []
    xs = np.asarray(inp["x_sample"]); xp = np.asarray(inp["x_prompt"])
    sd = np.asarray(inp["state_delta"]); c = np.asarray(inp["c"]); cc = np.asarray(inp["c_ctx"])
    for b in range(4):
        m = dict(shared)
        m["x"] = f(xs[b])
        m["cvec"] = f(c[b].reshape(8, 128).T)
        fl = np.zeros((128, 2), np.float32); fl[:, 0] = 1.0
        m["flags"] = fl
        ini = np.zeros((2, 8, 4, 128, 64), np.float32)
        ini[0, 0] = sd[b, 0, 0].reshape(4, 128, 64)
        ini[1, 7] = sd[b, 0, 1].reshape(4, 128, 64)
        m["init_s"] = ini
        maps.append(m)
    for j in range(4):
        m = dict(shared)
        m["x"] = f(xp[8 * j:8 * j + 8].reshape(TOK, DM))
        m["cvec"] = f(cc.reshape(8, 128).T)
        fl = np.zeros((128, 2), np.float32); fl[:, 1] = 1.0
        m["flags"] = fl
        m["init_s"] = np.zeros((2, 8, 4, 128, 64), np.float32)
        maps.append(m)
    return maps


_NC_CACHE = {}


def kernel(**inputs):
    if "nc" not in _NC_CACHE:
        _NC_CACHE["nc"] = build_program()
    nc = _NC_CACHE["nc"]
    maps = make_in_maps(inputs)
    res = run_bass_kernel_spmd(nc, maps, core_ids=list(range(8))).results
    y_s = np.stack([res[b]["y"] for b in range(4)], axis=0).astype(np.float32)
    y_p = np.concatenate([res[4 + j]["y"].reshape(8, 256, DM) for j in range(4)], axis=0).astype(np.float32)
    st = np.zeros((32, 1, 2, 8, 64, 64), np.float32)
    for j in range(4):
        so = res[4 + j]["s_out"]
        st[8 * j:8 * j + 8, 0] = so.reshape(2, 8, 8, 64, 64).transpose(1, 0, 2, 3, 4)
    return (y_p, y_s, st)
```

```python
import numpy as np
from concourse.bass_utils import run_bass_kernel_spmd
import concourse.bass as bass
import concourse.mybir as mybir
from contextlib import ExitStack

F32 = mybir.dt.float32
BF16 = mybir.dt.bfloat16
AF = mybir.ActivationFunctionType
ALU = mybir.AluOpType
AX = mybir.AxisListType


class Buf:
    __slots__ = ("name", "w", "r", "dsem", "dcnt")

    def __init__(self, name):
        self.name = name
        self.w = []
        self.r = []
        self.dsem = None
        self.dcnt = 0


class KB:
    COMPUTE = ("pe", "act", "dve", "pool")
    NAMES = {}

    def __init__(self, nc, es, strict_same=True):
        self.nc = nc
        self.es = es
        self.strict_same = strict_same
        self.eng = dict(pe=nc.tensor, act=nc.scalar, dve=nc.vector, pool=nc.gpsimd, sp=nc.sync)
        self.q = {e: [] for e in self.eng}
        self.sem = {e: es.enter_context(nc.semaphore("c_" + e)) for e in self.COMPUTE}
        self.cnt = {e: 0 for e in self.COMPUTE}
        self.waited = {e: {} for e in self.eng}
        self.pending_noinc = {e: False for e in self.COMPUTE}
        self.nsem = 4
        self.all_dma_bufs = []

    def buf(self, name):
        return Buf(name)

    def bufs(self, name, n):
        return [Buf(f"{name}{i}") for i in range(n)]

    def _waits(self, e, reads, writes, nosame=False, skip_sem=None):
        deps = []
        for b in reads:
            deps += b.w
        for b in writes:
            deps += b.w
            deps += b.r
        need = {}
        for (s, v, owner) in deps:
            if owner == e and (nosame or not self.strict_same or e == "pe"):
                continue
            if skip_sem is not None and s.num == skip_sem:
                continue
            if self.waited[e].get(s.num, 0) >= v:
                continue
            if need.get(s.num, (None, 0))[1] < v:
                need[s.num] = (s, v)
        out = []
        for num, (s, v) in need.items():
            self.waited[e][num] = v
            out.append((s, v))
        return out

    def op(self, e, fn, reads=(), writes=(), inc=True, nosame=False):
        waits = self._waits(e, reads, writes, nosame=nosame)
        sem = self.sem[e]
        import sys as _sys
        _fr = _sys._getframe(1)
        while _fr is not None and _fr.f_code.co_name in ("act", "tt", "stt", "ts", "cp", "mm", "op"):
            _fr = _fr.f_back
        _ln = _fr.f_lineno if _fr is not None else -1
        if inc:
            self.cnt[e] += 1
            tokv = self.cnt[e]
            self.pending_noinc[e] = False
        else:
            tokv = self.cnt[e] + 1
            self.pending_noinc[e] = True
        tok = (sem, tokv, e)

        def emit(engine, waits=waits, fn=fn, inc=inc, sem=sem, _ln=_ln):
            for (s, v) in waits:
                engine.wait_ge(s, v)
            ins = fn(engine)
            try:
                KB.NAMES[str(ins.ins.name)] = _ln
            except Exception:
                pass
            if inc:
                ins.then_inc(sem, 1)
        self.q[e].append(emit)
        for b in writes:
            b.w = [tok]
            b.r = []
        for b in reads:
            if b not in writes:
                b.r.append(tok)
        return tok

    def dma(self, qe, out, in_, reads=(), writes=(), track=None, partial=False):
        tb = track if track is not None else (writes[0] if writes else reads[0])
        waits = self._waits(qe, reads, writes,
                            skip_sem=(tb.dsem.num if (partial and tb.dsem is not None) else None))
        if tb.dsem is None:
            tb.dsem = self.es.enter_context(self.nc.semaphore(f"d{self.nsem}_" + tb.name))
            self.nsem += 1
            self.all_dma_bufs.append(tb)
        tb.dcnt += 16
        tok = (tb.dsem, tb.dcnt, "dma")

        def emit(engine, waits=waits, out=out, in_=in_, s=tb.dsem):
            for (sm, v) in waits:
                engine.wait_ge(sm, v)
            engine.dma_start(out=out, in_=in_).then_inc(s, 16)
        self.q[qe].append(emit)
        for b in writes:
            b.w = [tok]
            b.r = []
        for b in reads:
            if b not in writes:
                b.r.append(tok)
        return tok

    def wait_tokens(self, e, toks):
        need = []
        for (s, v, owner) in toks:
            if self.waited[e].get(s.num, 0) >= v:
                continue
            self.waited[e][s.num] = v
            need.append((s, v))

        def emit(engine, need=need):
            for (s, v) in need:
                engine.wait_ge(s, v)
        self.q[e].append(emit)

    def barrier(self):
        toks = [(self.sem[e], self.cnt[e], e) for e in self.COMPUTE if self.cnt[e] > 0]
        toks += [(b.dsem, b.dcnt, "dma") for b in self.all_dma_bufs]
        for e in self.eng:
            assert not self.pending_noinc.get(e, False)
            self.wait_tokens(e, toks)

    def flush(self):
        for e in self.COMPUTE:
            assert not self.pending_noinc[e], f"engine {e} ends with a non-incrementing op"
        q = self.q
        with self.nc.Block() as block:
            @block.sync
            def _(eng):
                for f in q["sp"]:
                    f(eng)

            @block.tensor
            def _(eng):
                for f in q["pe"]:
                    f(eng)

            @block.scalar
            def _(eng):
                for f in q["act"]:
                    f(eng)

            @block.vector
            def _(eng):
                for f in q["dve"]:
                    f(eng)

            @block.gpsimd
            def _(eng):
                for f in q["pool"]:
                    f(eng)
        self.q = {e: [] for e in self.eng}

    def finish(self):
        self.flush()


DM = 1024
TOK = 2048
NT = 16
NTT = 4
N_IN = 5664
DFF = 2816
Z0, BA0, SC0, CG0, XS0, GA0, GB0 = 1536, 2048, 2080, 2592, 3104, 3616, 4640
EPS = 1e-6
IDENT, MFI, MFS, MBI, MBS, BONES, ONES, BD32, M1I, M2I, NEGTF, NEGTB = range(12)
NMAT = 12


def build_program(stage=99):
    nc = bass.Bass("TRN2", target_bir_lowering=False)
    di = lambda n, s: nc.dram_tensor(n, s, F32, kind="ExternalInput").ap()
    x = di("x", [TOK, DM])
    cvec = di("cvec", [128, 8])
    flags = di("flags", [128, 2])
    init_s = di("init_s", [2, 8, 4, 128, 64])
    w_mod = di("w_mod", [DM, 6 * DM])
    b_mod = di("b_mod", [6 * DM])
    norm_gains = di("norm_gains", [4, DM])
    w_in = di("w_in", [DM, N_IN])
    cwq_d = di("cwq", [128, 12, 3])
    cws_d = di("cws", [128, 4, 3])
    a_log = di("a_log", [16])
    dt_bias = di("dt_bias", [16])
    onorm = di("onorm", [64])
    w_br_a = di("w_br_a", [512, DM])
    w_br_b = di("w_br_b", [512, DM])
    w_out = di("w_out", [DM, DM])
    w_ffn_in = di("w_ffn_in", [DM, 2 * DFF])
    w_ffn_out = di("w_ffn_out", [DFF, DM])
    cmat = di("cmat", [NMAT, 128, 128])
    y = nc.dram_tensor("y", [TOK, DM], F32, kind="ExternalOutput").ap()
    s_out = nc.dram_tensor("s_out", [2, 8, 4, 128, 64], F32, kind="ExternalOutput").ap()
    dbg = None
    if stage < 99:
        dbg = nc.dram_tensor("dbg", [128, 16384], F32, kind="ExternalOutput").ap()

    with ExitStack() as es:
        kb = KB(nc, es)
        T = lambda n, shape, dt=F32: es.enter_context(nc.sbuf_tensor(n, shape, dt))
        op = kb.op

        def act(out, in_, func, reads, writes, **kw):
            return op("act", lambda e: e.activation(out=out, in_=in_, func=func, **kw), reads, writes)

        def tt(eng, out, in0, in1, alu, reads, writes):
            return op(eng, lambda e: e.tensor_tensor(out=out, in0=in0, in1=in1, op=alu), reads, writes)

        def stt(eng, out, in0, scalar, in1, op0, op1, reads, writes):
            return op(eng, lambda e: e.scalar_tensor_tensor(out=out, in0=in0, scalar=scalar, in1=in1,
                                                            op0=op0, op1=op1), reads, writes)

        def ts(eng, out, in0, s1, s2, op0, op1, reads, writes):
            if s2 is None:
                return op(eng, lambda e: e.tensor_scalar(out=out, in0=in0, scalar1=s1, scalar2=None, op0=op0),
                          reads, writes)
            return op(eng, lambda e: e.tensor_scalar(out=out, in0=in0, scalar1=s1, scalar2=s2, op0=op0, op1=op1),
                      reads, writes)

        def cp(eng, out, in_, reads, writes):
            if eng == "act":
                return act(out, in_, AF.Copy, reads, writes)
            return op(eng, lambda e: e.tensor_copy(out=out, in_=in_), reads, writes)

        def mm(out, lhsT, rhs, start, stop, reads, writes, inc=None):
            return op("pe", lambda e: e.matmul(out, lhsT=lhsT, rhs=rhs, start=start, stop=stop),
                      reads, writes, inc=(stop if inc is None else inc))

        def prefetched(loaders, depth=1):
            issued = []

            def get(i):
                while len(issued) < min(len(loaders), i + 1 + depth):
                    issued.append(loaders[len(issued)]())
                return issued[i]
            return get

        def pipeline(gens, depth):
            gens = iter(gens)
            active = []
            done = False
            while True:
                while not done and len(active) < depth:
                    try:
                        active.append(next(gens))
                    except StopIteration:
                        done = True
                if not active:
                    break
                for g in list(active):
                    try:
                        next(g)
                    except StopIteration:
                        active.remove(g)

        dbg_state = dict(off=0, tok=None)
        if stage < 99:
            dstage = T("dstage", [128, 1024]); b_dstage = kb.buf("dstage"); b_dbg = kb.buf("dbgout")

        def dump(ap, width, reads):
            view = dstage[:, 0:width]
            if len(ap.shape) == 3:
                view = view.rearrange("p (a b) -> p a b", a=ap.shape[1])
            cp("dve", view, ap, reads, [b_dstage])
            o = dbg_state["off"]
            dbg_state["tok"] = kb.dma("sp", dbg[:, o:o + width], dstage[:, 0:width], reads=[b_dstage], track=b_dbg)
            dbg_state["off"] = o + width

        def dump_done():
            kb.wait_tokens("sp", [dbg_state["tok"]])
            kb.finish()

        PS = es.enter_context(nc.psum_tensor("ps", [128, 8, 512], F32))
        pb = kb.bufs("psb", 8)

        cm_b = T("cm_b", [128, NMAT, 128], BF16); b_cmb = kb.buf("cmb")
        flg = T("flg", [128, 2]); b_flg = kb.buf("flg")
        eps_t = T("eps_t", [128, 3]); b_eps = kb.buf("eps")
        NWB = 3 if stage >= 99 else 2
        KB_ = 1024
        AR_BYTES = (203 if stage >= 99 else 199) * KB_
        AR = T("arena", [128, AR_BYTES // 2], BF16)

        class Region:
            def __init__(self, start, end):
                self.start, self.end, self.pos = start, end, start

            def take(self, shape, dt=F32):
                n = 1
                for s_ in shape[1:]:
                    n *= s_
                nbytes = n * (4 if dt == F32 else 2)
                nbytes = (nbytes + 63) // 64 * 64
                assert self.pos + nbytes <= self.end, (self.pos, nbytes, self.end, shape)
                v = AR[:, self.pos // 2:(self.pos + n * (4 if dt == F32 else 2)) // 2]
                self.pos += nbytes
                if dt == F32:
                    v = v.bitcast(F32)
                if len(shape) == 3:
                    v = v.rearrange("p (a b) -> p a b", a=shape[1])
                elif len(shape) == 4:
                    v = v.rearrange("p (a b c) -> p a b c", a=shape[1], b=shape[2])
                return v

        RP = Region(0, 40 * KB_)
        RA = Region(40 * KB_, 88 * KB_)
        RB = Region(88 * KB_, 104 * KB_)
        RC = Region(104 * KB_, 110 * KB_)
        D0 = 110 * KB_
        modr = RP.take([128, 6, DM]); b_modr = kb.bufs("modr", 6)
        WBH = {}
        b_wb = kb.bufs("wb", 3)
        wb_ctr = [0]

        def wload(src_ap, ncols=512, parts=None):
            WB = WBH['wb']
            i = wb_ctr[0] % len(WB)
            wb_ctr[0] += 1
            if parts is None:
                nk_ = src_ap.shape[0] // 128
                kb.dma("pool", WB[i][:, 0:nk_, 0:ncols], src_ap.rearrange("(kc p) c -> p kc c", p=128), writes=[b_wb[i]])
            else:
                off = 0
                for k, (ap, w) in enumerate(parts):
                    kb.dma("pool", WB[i][:, :, off:off + w], ap.rearrange("(kc p) c -> p kc c", p=128),
                           writes=[b_wb[i]], partial=(k > 0))
                    off += w
            return WB[i], b_wb[i]

        identb = cm_b[:, IDENT, :]
        idf = T("idf", [128, 128]); b_idf = kb.buf("idf")
        kb.dma("sp", idf[:], cmat[IDENT], writes=[b_idf])

        kb.dma("pool", cm_b[:], cmat.rearrange("m p c -> p m c"), writes=[b_cmb])
        kb.dma("sp", flg[:], flags, writes=[b_flg])
        op("dve", lambda e: e.memset(eps_t[:, 0:1], EPS), writes=[b_eps])
        op("dve", lambda e: e.memset(eps_t[:, 1:2], 1.0), writes=[b_eps])
        op("dve", lambda e: e.memset(eps_t[:, 2:3], 64.0 * EPS), writes=[b_eps])

        ysc = RP.take([128, 4, TOK], BF16); b_ysc = kb.bufs("ysc", 16)
        T12 = lambda n, shape, dt=F32: RC.take(shape, dt)
        qT = RA.take([128, 4, TOK], BF16); b_qT = kb.bufs("qT", 16)
        kT = RA.take([128, 4, TOK], BF16); b_kT = kb.bufs("kT", 16)
        v_tok = RA.take([128, NT, 512], BF16); b_vt = kb.bufs("vt", 16)
        zs = RB.take([128, NT, 512], BF16); b_zs = kb.bufs("zs", 16)
        bet = T12("bet", [128, NT, 16]); b_bet = kb.buf("bet")
        nbet = T12("nbet", [128, NT, 16]); b_nbet = kb.buf("nbet")
        gg = T12("gg", [128, NT, 16]); b_gg = kb.buf("gg")
        gbf = T12("gbf", [128, NT, 16], BF16); b_gbf = kb.buf("gbf")
        egc = T12("egc", [128, NT, 16]); b_egc = kb.buf("egc")
        ekt = T12("ekt", [128, NT, 16]); b_ekt = kb.buf("ekt")
        eglp = T12("eglp", [128, NT, 2, 4]); b_eglp = kb.buf("eglp")
        cwq = T("cwq_s", [128, 12, 3]); b_cwq = kb.buf("cwq")
        cwqp = T("cwqp", [128, 12, 3]); b_cwqp = kb.buf("cwqp")
        cws = T("cws_s", [128, 4, 3]); b_cws = kb.buf("cws")
        cwsp = T("cwsp", [128, 4, 3]); b_cwsp = kb.buf("cwsp")
        kb.dma("sp", cwq[:], cwq_d, writes=[b_cwq])
        kb.dma("sp", cws[:], cws_d, writes=[b_cws])
        ts("dve", cwqp[:], cwq[:], flg[:, 1:2], None, ALU.mult, None, [b_cwq, b_flg], [b_cwqp])
        ts("dve", cwsp[:], cws[:], flg[:, 1:2], None, ALU.mult, None, [b_cws, b_flg], [b_cwsp])

        RD = Region(D0, AR_BYTES)
        WBH['wb'] = [RD.take([128, 8, 512], BF16) for i in range(NWB)]
        D1 = RD.pos
        if True:
            R0 = Region(D1, AR_BYTES)
            T0 = lambda n, shape, dt=F32: R0.take(shape, dt)
            c_sb = T0("c_sb", [128, 8]); b_c = kb.buf("c_sb")
            cs = T0("cs", [128, 8]); b_cs = kb.buf("cs")
            scb = T0("scb", [128, 8, 128], BF16); b_scb = kb.buf("scb")
            bmr = T0("bmr", [128, 2, DM]); b_bmr = kb.bufs("bmr", 2)
            gr = T0("gr", [128, 4, DM]); b_gr = kb.bufs("gr", 4)
            tmpm = T0("tmpm", [128, DM]); b_tmpm = kb.buf("tmpm")
            kb.dma("sp", c_sb[:], cvec, writes=[b_c])
            for i in range(4):
                kb.dma("sp", gr[:, i, :], norm_gains[i].partition_broadcast(128), writes=[b_gr[i]])
            act(cs[:], c_sb[:], AF.Silu, [b_c], [b_cs])
            cp("dve", scb[:], cs[:].unsqueeze(2).to_broadcast([128, 8, 128]), [b_cs], [b_scb])
            for s in range(6):
                bs = s % 2
                kb.dma("sp", bmr[:, bs, :], b_mod[s * DM:(s + 1) * DM].partition_broadcast(128), writes=[b_bmr[bs]])
                for nb in range(2):
                    wt, bw = wload(w_mod[:, s * DM + nb * 512: s * DM + (nb + 1) * 512])
                    bank = 2 * bs + nb
                    for kc in range(8):
                        mm(PS[:, bank, :], scb[:, kc, :], wt[:, kc, :], kc == 0, kc == 7, [b_scb, bw], [pb[bank]])
                ps2 = PS[:, 2 * bs:2 * bs + 2, :].rearrange("p a b -> p (a b)")
                rd = [pb[2 * bs], pb[2 * bs + 1], b_bmr[bs]]
                kind = s % 3
                slot = [0, 1, 2, 3, 4, 5][s]
                if kind == 0:
                    tt("dve", modr[:, slot, :], ps2, bmr[:, bs, :], ALU.add, rd, [b_modr[slot]])
                elif kind == 1:
                    gi = 0 if s == 1 else 2
                    tt("dve", tmpm[:], ps2, bmr[:, bs, :], ALU.add, rd, [b_tmpm])
                    stt("dve", modr[:, slot, :], tmpm[:], 1.0, gr[:, gi, :], ALU.add, ALU.mult,
                        [b_tmpm, b_gr[gi]], [b_modr[slot]])
                else:
                    gi = 1 if s == 2 else 3
                    tt("dve", tmpm[:], ps2, bmr[:, bs, :], ALU.add, rd, [b_tmpm])
                    tt("dve", modr[:, slot, :], tmpm[:], gr[:, gi, :], ALU.mult, [b_tmpm, b_gr[gi]], [b_modr[slot]])
            kb.barrier()
            kb.flush()
        B1r, A1r, C1r, B2r, A2r, C2r = [modr[:, i, :] for i in range(6)]

        if stage == 0:
            kb.barrier()
            for i in range(6):
                dump(modr[:, i, :], 1024, [b_modr[i]])
            dump_done()
            return nc


        if True:
            R1 = Region(D1, AR_BYTES)
            T1 = lambda n, shape, dt=F32: R1.take(shape, dt)
            hT = T1("hT", [128, 8, TOK], BF16); b_hT = kb.bufs("hT", 16)
            xst = [T1(f"xst{i}", [128, DM]) for i in range(2)]; b_xst = kb.bufs("xst", 2)
            ssq = T1("ssq", [128, NT]); b_ssq = kb.bufs("ssq", 16)
            sdv = T1("sdv", [128, NT]); b_sdv = kb.bufs("sdv", 16)
            rstd = T1("rstd", [128, NT]); b_rstd = kb.bufs("rstd", 16)
            t1 = [T1("t1_0", [128, DM])] * 2; b_t1 = [kb.buf("t1_")] * 2
            htok = [T1(f"htok{i}", [128, DM], BF16) for i in range(2)]; b_htok = kb.bufs("htok", 2)
            wba = T1("wba", [128, 8, 32], BF16); b_wba = kb.buf("wba")
            dtb = T1("dtb", [128, 16]); b_dtb = kb.buf("dtb")
            alr = T1("alr", [128, 16]); b_alr = kb.buf("alr")
            nA = T1("nA", [128, 16]); b_nA = kb.buf("nA")
            sp = [T1(f"sp{i}", [128, NT, 16]) for i in range(4)]; b_sp = kb.bufs("sp", 4)
            gcs = T1("gcs", [128, NT, 32]); b_gcs = kb.buf("gcs")
            ycv = [T1(f"ycv{i}", [128, 512]) for i in range(2)]; b_ycv = kb.bufs("ycv", 2)
            ysb = [T1(f"ysb{i}", [128, 512], BF16) for i in range(2)]; b_ysb = kb.bufs("ysb", 2)
            sqb = [T1(f"sqb{i}", [128, 512], BF16) for i in range(2)]; b_sqb = kb.bufs("sqb", 2)
            sdn = [T1(f"sdn{i}", [128, 512]) for i in range(2)]; b_sdn = kb.bufs("sdn", 2)
            xss = sdn; b_xss = b_sdn

            kb.dma("pool", wba[:], w_in[:, BA0:BA0 + 32].rearrange("(kc p) c -> p kc c", p=128), writes=[b_wba])
            kb.dma("sp", dtb[:], dt_bias.partition_broadcast(128), writes=[b_dtb])
            kb.dma("sp", alr[:], a_log.partition_broadcast(128), writes=[b_alr])

            def it_1a(n):
                i2 = n % 2
                kb.dma("sp", xst[i2][:], x[n * 128:(n + 1) * 128, :], writes=[b_xst[i2]])
                act(htok[i2][:], xst[i2][:], AF.Square, [b_xst[i2]], [b_htok[i2], b_ssq[n]], accum_out=ssq[:, n:n + 1])
                act(sdv[:, n:n + 1], ssq[:, n:n + 1], AF.Sqrt, [b_ssq[n], b_eps], [b_sdv[n]],
                    scale=1.0 / DM, bias=eps_t[:, 0:1])
                yield
                op("dve", lambda e, n=n: e.reciprocal(out=rstd[:, n:n + 1], in_=sdv[:, n:n + 1]), [b_sdv[n]], [b_rstd[n]])
                stt("dve", xst[i2][:], xst[i2][:], rstd[:, n:n + 1], A1r, ALU.mult, ALU.mult,
                    [b_xst[i2], b_rstd[n], b_modr[1]], [b_xst[i2]])
                yield
                tt("dve", htok[i2][:], xst[i2][:], B1r, ALU.add, [b_xst[i2], b_modr[0]], [b_htok[i2]])
                yield
                bk = 2 * i2
                for kc in range(8):
                    mm(PS[:, bk + kc // 4, (kc % 4) * 128:(kc % 4 + 1) * 128], htok[i2][:, kc * 128:(kc + 1) * 128], identb,
                       True, True, [b_htok[i2], b_cmb], [pb[bk + kc // 4]], inc=(kc % 4 == 3))
                yield
                cp("act", hT[:, :, n * 128:(n + 1) * 128],
                   PS[:, bk:bk + 2, :].rearrange("p a (c t) -> p (a c) t", t=128), [pb[bk], pb[bk + 1]], [b_hT[n]])
                yield
            pipeline((it_1a(n) for n in range(NT)), 2)

            if stage == 1:
                kb.barrier()
                for kc in range(8):
                    dump(hT[:, kc, 0:1024], 1024, b_hT)
                dump_done()
                return nc

            wz, b_wz = wload(w_in[:, Z0:Z0 + 512])
            for n in range(NT):
                bk = n % 2
                for kc in range(8):
                    mm(PS[:, bk, :], hT[:, kc, n * 128:(n + 1) * 128], wz[:, kc, :], kc == 0, kc == 7,
                       [b_hT[n], b_wz], [pb[bk]])
                act(zs[:, n, :], PS[:, bk, :], AF.Silu, [pb[bk]], [b_zs[n]])
                for kc in range(8):
                    mm(PS[:, 7, n * 32:(n + 1) * 32], hT[:, kc, n * 128:(n + 1) * 128], wba[:, kc, :], kc == 0, kc == 7,
                       [b_hT[n], b_wba], [pb[7]])
            pba = PS[:, 7, :].rearrange("p (n c) -> p n c", c=32)
            a1, ab, e1, r1 = sp
            act(bet[:], pba[:, :, 0:16], AF.Sigmoid, [pb[7]], [b_bet])
            ts("dve", nbet[:], bet[:], -1.0, None, ALU.mult, None, [b_bet], [b_nbet])
            tt("dve", a1[:], pba[:, :, 16:32], dtb[:].unsqueeze(1).to_broadcast([128, NT, 16]), ALU.add,
               [pb[7], b_dtb], [b_sp[0]])
            act(ab[:], a1[:], AF.Abs, [b_sp[0]], [b_sp[1]])
            act(e1[:], ab[:], AF.Exp, [b_sp[1]], [b_sp[2]], scale=-1.0)
            act(e1[:], e1[:], AF.Ln, [b_sp[2], b_eps], [b_sp[2]], bias=eps_t[:, 1:2])
            ts("dve", r1[:], a1[:], 0.0, None, ALU.max, None, [b_sp[0]], [b_sp[3]])
            tt("dve", r1[:], r1[:], e1[:], ALU.add, [b_sp[3], b_sp[2]], [b_sp[3]])
            act(nA[:], alr[:], AF.Exp, [b_alr], [b_nA])
            ts("dve", nA[:], nA[:], -1.0, None, ALU.mult, None, [b_nA], [b_nA])
            tt("dve", gg[:], r1[:], nA[:].unsqueeze(1).to_broadcast([128, NT, 16]), ALU.mult, [b_sp[3], b_nA], [b_gg])
            cp("dve", gbf[:], gg[:], [b_gg], [b_gbf])
            pgc = PS[:, 6, :].rearrange("p (n c) -> p n c", c=32)
            for n in range(NT):
                mm(pgc[:, n, 0:8], cm_b[:, MFI, :], gbf[:, n, 0:8], True, True, [b_cmb, b_gbf], [pb[6]], inc=False)
                mm(pgc[:, n, 8:16], cm_b[:, MBI, :], gbf[:, n, 8:16], True, True, [b_cmb, b_gbf], [pb[6]], inc=False)
                mm(pgc[:, n, 16:32], cm_b[:, ONES, :], gbf[:, n, 0:16], True, True, [b_cmb, b_gbf], [pb[6]],
                   inc=(n == NT - 1))
            cp("act", gcs[:], pgc, [pb[6]], [b_gcs])
            act(egc[:], gcs[:, :, 0:16], AF.Exp, [b_gcs], [b_egc])
            tt("dve", ab[:], gcs[:, :, 16:32], gcs[:, :, 0:16], ALU.subtract, [b_gcs], [b_sp[1]])
            act(ekt[:], ab[:], AF.Exp, [b_sp[1]], [b_ekt])
            act(a1[:], gcs[:, :, 16:32], AF.Exp, [b_gcs], [b_sp[0]])
            a5 = a1[:].rearrange("p n (d q h) -> p n d q h", d=2, q=4)
            cp("dve", eglp[0:64], a5[0:64, :, :, :, 0], [b_sp[0]], [b_eglp])
            cp("dve", eglp[64:128], a5[64:128, :, :, :, 1], [b_sp[0]], [b_eglp])

            if stage == 15:
                kb.barrier()
                dump(zs[:, 14:16, :], 1024, b_zs)
                dump(bet[:], 256, [b_bet])
                dump(gg[:], 256, [b_gg])
                dump(egc[:], 256, [b_egc])
                dump(ekt[:], 256, [b_ekt])
                dump(eglp[:].rearrange("p n d q -> p n (d q)"), 128, [b_eglp])
                dump_done()
                return nc

            def conv_g(src, dst, bdst, cwt, cwpt, fc, rd, eng):
                act(dst[:], src, AF.Copy, rd + [b_cwq, b_cws], [bdst], scale=cwt[:, fc, 1:2])
                yield
                d3 = dst[:].rearrange("p (s c) -> p s c", c=64)
                s3 = src.rearrange("p (s c) -> p s c", c=64)
                stt(eng, d3[:, :, 1:64], s3[:, :, 0:63], cwt[:, fc, 0:1], d3[:, :, 1:64], ALU.mult, ALU.add,
                    rd + [bdst], [bdst])
                stt(eng, d3[:, :, 0:63], s3[:, :, 1:64], cwt[:, fc, 2:3], d3[:, :, 0:63], ALU.mult, ALU.add,
                    rd + [bdst], [bdst])
                d4 = dst[:].rearrange("p (q s c) -> p q s c", q=2, s=4)
                s4 = src.rearrange("p (q s c) -> p q s c", q=2, s=4)
                stt(eng, d4[:, :, 1:4, 0:1], s4[:, :, 0:3, 63:64], cwpt[:, fc, 0:1], d4[:, :, 1:4, 0:1],
                    ALU.mult, ALU.add, rd + [bdst, b_cwqp, b_cwsp], [bdst])
                stt(eng, d4[:, :, 0:3, 63:64], s4[:, :, 1:4, 0:1], cwpt[:, fc, 2:3], d4[:, :, 0:3, 63:64],
                    ALU.mult, ALU.add, rd + [bdst], [bdst])
                yield

            def it_qkv(blk, c, t4, rot, wt, bw):
                fc = blk * 4 + c
                bank = rot % 4
                i2 = rot % 2
                tsl = slice(t4 * 512, (t4 + 1) * 512)
                for kc in range(8):
                    mm(PS[:, bank, :], wt[:, kc, c * 128:(c + 1) * 128], hT[:, kc, tsl], kc == 0, kc == 7,
                       [bw] + b_hT[4 * t4:4 * t4 + 4], [pb[bank]])
                yield
                for _ in conv_g(PS[:, bank, :], ycv[i2], b_ycv[i2], cwq, cwqp, fc, [pb[bank]], "dve"):
                    yield
                if blk == 2:
                    act(ysb[i2][:], ycv[i2][:], AF.Silu, [b_ycv[i2]], [b_ysb[i2]])
                    yield
                    pbk = 4 + i2
                    for j in range(4):
                        mm(PS[:, pbk, j * 128:(j + 1) * 128], ysb[i2][:, j * 128:(j + 1) * 128], identb,
                           True, True, [b_ysb[i2], b_cmb], [pb[pbk]], inc=(j == 3))
                    yield
                    cp("act", v_tok[:, t4 * 4:(t4 + 1) * 4, c * 128:(c + 1) * 128],
                       PS[:, pbk, :].rearrange("p (j f) -> p j f", f=128), [pb[pbk]], b_vt[4 * t4:4 * t4 + 4])
                    yield
                else:
                    act(ycv[i2][:], ycv[i2][:], AF.Silu, [b_ycv[i2]], [b_ycv[i2]])
                    yield
                    tt("pool", sqb[i2][:], ycv[i2][:], ycv[i2][:], ALU.mult, [b_ycv[i2]], [b_sqb[i2]])
                    yield
                    pbk = 6 + i2
                    mm(PS[:, pbk, :], cm_b[:, BONES, :], sqb[i2][:], True, True, [b_cmb, b_sqb[i2]], [pb[pbk]])
                    yield
                    if blk == 0:
                        act(sdn[i2][:], PS[:, pbk, :], AF.Sqrt, [pb[pbk], b_eps], [b_sdn[i2]], scale=64.0, bias=eps_t[:, 2:3])
                    else:
                        act(sdn[i2][:], PS[:, pbk, :], AF.Sqrt, [pb[pbk], b_eps], [b_sdn[i2]], bias=eps_t[:, 0:1])
                    yield
                    op("dve", lambda e, i2=i2: e.reciprocal(out=sdn[i2][:], in_=sdn[i2][:]),
                       [b_sdn[i2]], [b_sdn[i2]])
                    yield
                    dstT, bdT = (qT, b_qT) if blk == 0 else (kT, b_kT)
                    tt("pool", dstT[:, c, tsl], ycv[i2][:], sdn[i2][:], ALU.mult, [b_ycv[i2], b_sdn[i2]], bdT[4 * t4:4 * t4 + 4])
                    yield

            def gen_qkv():
                rot = 0
                getw = prefetched([(lambda blk=blk: wload(w_in[:, blk * 512:(blk + 1) * 512])) for blk in range(3)])
                for blk in range(3):
                    wt, bw = getw(blk)
                    for c in range(4):
                        for t4 in range(NTT):
                            yield it_qkv(blk, c, t4, rot, wt, bw)
                            rot += 1
            pipeline(gen_qkv(), 2)
            rot = 48
            if stage == 18:
                kb.barrier()
                dump(qT[:, 0, 0:1024], 1024, b_qT)
                dump(kT[:, 1, 1024:2048], 1024, b_kT)
                dump(v_tok[:, 0:2, :], 1024, b_vt)
                dump_done()
                return nc
            def it_sc(c, t4, rot, wt, bw):
                i2 = rot % 2
                bs = (0, 1, 2) if i2 == 0 else (3, 4, 5)
                tsl = slice(t4 * 512, (t4 + 1) * 512)
                for g3 in range(3):
                    for kc in range(8):
                        mm(PS[:, bs[g3], :], wt[:, kc, g3 * 128:(g3 + 1) * 128], hT[:, kc, tsl], kc == 0, kc == 7,
                           [bw] + b_hT[4 * t4:4 * t4 + 4], [pb[bs[g3]]])
                yield
                cp("act", xss[i2][:], PS[:, bs[2], :], [pb[bs[2]]], [b_xss[i2]])
                yield
                tt("dve", xss[i2][:], PS[:, bs[1], :], xss[i2][:], ALU.mult, [pb[bs[1]], b_xss[i2]], [b_xss[i2]])
                yield
                for _ in conv_g(xss[i2][:], ycv[i2], b_ycv[i2], cws, cwsp, c, [b_xss[i2]], "dve"):
                    yield
                tt("dve", ysc[:, c, tsl], PS[:, bs[0], :], ycv[i2][:], ALU.mult, [pb[bs[0]], b_ycv[i2]],
                   b_ysc[4 * t4:4 * t4 + 4])
                yield

            def gen_sc():
                rot = 0
                getw = prefetched([(lambda c=c: wload(None, parts=[(w_in[:, SC0 + c * 128:SC0 + (c + 1) * 128], 128),
                                                                   (w_in[:, CG0 + c * 128:CG0 + (c + 1) * 128], 128),
                                                                   (w_in[:, XS0 + c * 128:XS0 + (c + 1) * 128], 128)]))
                                   for c in range(4)])
                for c in range(4):
                    wt, bw = getw(c)
                    for t4 in range(NTT):
                        yield it_sc(c, t4, rot, wt, bw)
                        rot += 1
            pipeline(gen_sc(), 2)

            if stage == 21:
                kb.barrier()
                dump(ysc[:, 2, 512:1536], 1024, b_ysc)
                dump_done()
                return nc
            if stage == 2:
                kb.barrier()
                dump(qT[:, 0, 0:1024], 1024, b_qT)
                dump(kT[:, 1, 1024:2048], 1024, b_kT)
                dump(v_tok[:, 0:2, :], 1024, b_vt)
                dump(zs[:, 14:16, :], 1024, b_zs)
                dump(ysc[:, 2, 512:1536], 1024, b_ysc)
                dump(bet[:], 256, [b_bet])
                dump(gg[:], 256, [b_gg])
                dump(egc[:], 256, [b_egc])
                dump(ekt[:], 256, [b_ekt])
                dump(eglp[:].rearrange("p n d q -> p n (d q)"), 128, [b_eglp])
                dump_done()
                return nc
            kb.barrier()
            kb.flush()

        onT = qT; b_onT = b_qT
        if True:
            R2 = Region(D0, AR_BYTES)
            T2 = lambda n, shape, dt=F32: R2.take(shape, dt)
            oacc = T2("oacc", [128, NT, 512]); b_oacc = [kb.bufs(f"oacc{n}_", 2) for n in range(NT)]
            ident4 = T2("ident4", [128, 4, 128], BF16); b_id4 = kb.buf("ident4")
            cp("pool", ident4, identb.unsqueeze(1).to_broadcast([128, 4, 128]), [b_cmb], [b_id4])
            S = []
            for s in range(2):
                d_ = dict(
                    rg=T2(f"rg{s}", [128, 4, 128], BF16), Em=T2(f"Em{s}", [128, 4, 128]),
                    attnT=T2(f"attnT{s}", [128, 4, 128], BF16),
                    VP=[T2(f"VP{s}_{i}", [128, 4, 2, 128], BF16) for i in range(2)],
                    W=[T2(f"W{s}_{i}", [128, 4, 128], BF16) for i in range(2)],
                    PT=[T2(f"PT{s}_{i}", [128, 4, 128], BF16) for i in range(2)],
                    OFF=T2(f"OFF{s}", [128, 4, 3, 128], BF16),
                    T2T=T2(f"T2T{s}", [128, 4, 128], BF16), kqbd=T2(f"kqbd{s}", [128, 8, 128], BF16),
                    ksb=T2(f"ksb{s}", [128, 4, 64], BF16), kg=T2(f"kg{s}", [128, 4, 64], BF16),
                    kt=T2(f"kt{s}", [128, 4, 64], BF16), wT=T2(f"wT{s}", [128, 2, 128], BF16),
                    usb=T2(f"usb{s}", [128, 4, 64]), tmp=T2(f"tmp{s}", [128, 4, 64]),
                    vnb=T2(f"vnb{s}", [128, 4, 64], BF16),
                )
                d_["o1"] = d_["tmp"]
                eoff = R2.pos
                d_["E"] = T2(f"E{s}", [128, 4, 128])
                d_["Y"] = Region(eoff, eoff + 2048).take([128, 2, 4, 128], BF16)
                for nm in ["rg", "E", "Em", "attnT", "OFF", "T2T", "kqbd", "ksb", "kg", "kt", "wT", "usb", "tmp", "vnb"]:
                    d_["b_" + nm] = kb.buf(f"{nm}{s}")
                d_["b_o1"] = d_["b_tmp"]
                d_["b_V"] = kb.bufs(f"V{s}_", 2)
                d_["b_P"] = kb.bufs(f"P{s}_", 2)
                d_["b_W"] = kb.bufs(f"W{s}_", 2)
                d_["b_PT"] = kb.bufs(f"PT{s}_", 2)
                S.append(d_)
            Sm = [[T2(f"Sm{d}{hf}", [128, 2, 64]) for hf in range(2)] for d in range(2)]
            Sbd = [[T2(f"Sbd{d}{hf}", [128, 2, 128], BF16) for hf in range(2)] for d in range(2)]
            inis = [[T2(f"inis{d}{hf}", [128, 2, 64]) for hf in range(2)] for d in range(2)]
            b_Sm = [[kb.buf(f"Sm{d}{hf}") for hf in range(2)] for d in range(2)]
            b_Sbd = [[kb.buf(f"Sbd{d}{hf}") for hf in range(2)] for d in range(2)]
            b_inis = [[kb.buf(f"inis{d}{hf}") for hf in range(2)] for d in range(2)]
            b_sout = [[kb.buf(f"sout{d}{hf}") for hf in range(2)] for d in range(2)]
            op("pool", lambda e: e.memset(oacc, 0.0), [], [b for bb in b_oacc for b in bb])
            for s in range(2):
                op("pool", lambda e, s=s: e.memset(S[s]["kqbd"], 0.0), [], [S[s]["b_kqbd"]])
            for d in range(2):
                for hf in range(2):
                    op("pool", lambda e, d=d, hf=hf: e.memset(Sm[d][hf], 0.0), [], [b_Sm[d][hf]])
                    op("pool", lambda e, d=d, hf=hf: e.memset(Sbd[d][hf], 0.0), [], [b_Sbd[d][hf]])

            def bc(ap, shape, axis):
                return ap.unsqueeze(axis).to_broadcast(shape)

            def problem(d, n, hf):
                s = hf
                Z = S[s]
                base = 4 * s
                B0, B1, B2, B3 = [PS[:, base + i, :] for i in range(4)]
                p0, p1, p2b, p3 = [pb[base + i] for i in range(4)]
                tl = slice(n * 128, (n + 1) * 128)
                hs = slice(d * 8 + 4 * hf, d * 8 + 4 * hf + 4)
                M_rhs = cm_b[:, MFI if d == 0 else MBI, :]
                M_lhs = cm_b[:, MBS if d == 0 else MFS, :]
                M_inc = cm_b[:, MFI if d == 0 else MBI, :]
                M_str = cm_b[:, MFS if d == 0 else MBS, :]
                f3 = [128, 4, 128]
                v3 = lambda bank_ap: bank_ap.rearrange("p (h c) -> p h c", h=4)
                VP, W, PT, OFF, Y = Z["VP"], Z["W"], Z["PT"], Z["OFF"], Z["Y"]
                bV, bP, bW, bPT, bOFF, bE = Z["b_V"], Z["b_P"], Z["b_W"], Z["b_PT"], Z["b_OFF"], Z["b_E"]
                tt("dve", Z["rg"], bc(gg[:, n, hs], f3, 2), bc(M_rhs, f3, 1), ALU.mult, [b_gg, b_cmb], [Z["b_rg"]])
                if stage == 60:
                    yield "stop"
                mm(B2, M_lhs, Z["rg"].rearrange("p h c -> p (h c)"), True, False, [b_cmb, Z["b_rg"]], [p2b], inc=False)
                mm(B2, cm_b[:, NEGTF if d == 0 else NEGTB, :], ident4.rearrange("p h c -> p (h c)"), False, True,
                   [b_cmb, b_id4], [p2b])
                kq5 = Z["kqbd"].rearrange("p (q t hh) c -> p q t hh c", q=2, t=2)
                cp("act", kq5[0:64, :, 0, 0, :], kT[0:64, 2 * hf:2 * hf + 2, tl], [b_kT[n]], [Z["b_kqbd"]])
                cp("act", kq5[64:128, :, 0, 1, :], kT[64:128, 2 * hf:2 * hf + 2, tl], [b_kT[n]], [Z["b_kqbd"]])
                cp("act", kq5[0:64, :, 1, 0, :], qT[0:64, 2 * hf:2 * hf + 2, tl], [b_qT[n]], [Z["b_kqbd"]])
                cp("act", kq5[64:128, :, 1, 1, :], qT[64:128, 2 * hf:2 * hf + 2, tl], [b_qT[n]], [Z["b_kqbd"]])
                for q2 in range(2):
                    mm(PS[:, base + q2, :], kT[:, 2 * hf + q2, tl],
                       Z["kqbd"][:, 4 * q2:4 * q2 + 4, :].rearrange("p a c -> p (a c)"), True, True,
                       [b_kT[n], Z["b_kqbd"]], [pb[base + q2]])
                act(Z["Em"], v3(B2), AF.Exp, [p2b], [Z["b_Em"]])
                if ((n % 2 == 0) if d == 0 else (n % 2 == 1)):
                    sm_, sbd_, bsm_, bsbd_ = Sm[d][hf], Sbd[d][hf], b_Sm[d][hf], b_Sbd[d][hf]
                    kb.dma("sp", inis[d][hf][:], init_s[d, n // 2, 2 * hf:2 * hf + 2].rearrange("q p e -> p q e"),
                           writes=[b_inis[d][hf]])
                    stt("dve", sm_[:], sm_[:], flg[:, 0:1], inis[d][hf][:], ALU.mult, ALU.add, [bsm_, b_flg, b_inis[d][hf]], [bsm_])
                    cp("act", sbd_[0:64, :, 0:64], sm_[0:64, :, :], [bsm_], [bsbd_])
                    cp("act", sbd_[64:128, :, 64:128], sm_[64:128, :, :], [bsm_], [bsbd_])
                for q2 in range(2):
                    mm(B3[:, q2 * 128:(q2 + 1) * 128], kT[:, 2 * hf + q2, tl], identb, True, True, [b_kT[n], b_cmb], [p3],
                       inc=(q2 == 1))
                act(Z["ksb"][:], B3[:, 0:256].rearrange("p (h c) -> p h c", h=4), AF.Copy, [p3], [Z["b_ksb"]])
                tt("pool", Z["kg"][:], Z["ksb"][:], bc(egc[:, n, hs], [128, 4, 64], 2), ALU.mult, [Z["b_ksb"], b_egc], [Z["b_kg"]])
                tt("pool", Z["kt"][:], Z["ksb"][:], bc(ekt[:, n, hs], [128, 4, 64], 2), ALU.mult, [Z["b_ksb"], b_ekt], [Z["b_kt"]])
                yield
                s4 = lambda ap: ap.rearrange("p (a b) c -> p a b c", a=2)
                kk4 = PS[:, base:base + 2, 0:256].rearrange("p q (hh c) -> p q hh c", hh=2)
                kq4 = PS[:, base:base + 2, 256:512].rearrange("p q (hh c) -> p q hh c", hh=2)
                V0f = VP[1][:, :, 0, :]
                W0f = W[1]
                for h in range(4):
                    kk_h = PS[:, base + h // 2, (h % 2) * 128:(h % 2) * 128 + 128]
                    stt("dve", V0f[:, h, :], kk_h, nbet[:, n, d * 8 + 4 * hf + h:d * 8 + 4 * hf + h + 1], Z["Em"][:, h, :],
                        ALU.mult, ALU.mult, [pb[base + h // 2], b_nbet, Z["b_Em"]], [bV[1]])
                for h in range(4):
                    mm(B2[:, h * 128:(h + 1) * 128], V0f[:, h, :], identb, True, True, [bV[1], b_cmb], [p2b], inc=(h == 3))
                act(W0f, v3(B2), AF.Copy, [p2b], [bW[1]])
                tt("dve", s4(Z["attnT"]), kq4, s4(Z["Em"]), ALU.mult, [p0, p1, Z["b_Em"]], [Z["b_attnT"]])
                yield
                mBD, mM1, mM2 = [bc(cm_b[:, i, :], f3, 1) for i in (BD32, M1I, M2I)]
                tt("pool", VP[0][:, :, 0, :], V0f, mBD, ALU.mult, [bV[1], b_cmb], [bV[0]])
                tt("dve", OFF[:, :, 0, :], V0f, mM1, ALU.mult, [bV[1], b_cmb], [bOFF])
                tt("dve", W[0], W0f, mBD, ALU.mult, [bW[1], b_cmb], [bW[0]])
                tt("dve", OFF[:, :, 1, :], W0f, mM1, ALU.mult, [bW[1], b_cmb], [bOFF])
                tt("pool", OFF[:, :, 2, :], W0f, mM2, ALU.mult, [bW[1], b_cmb], [bOFF])
                tt("pool", VP[1][:, :, 1, :], VP[0][:, :, 0, :], bc(identb, f3, 1), ALU.add, [bV[0], b_cmb], [bP[1]])
                tt("pool", PT[1], W[0], bc(identb, f3, 1), ALU.add, [bW[0], b_cmb], [bPT[1]])
                yield
                idf3 = bc(idf[:], f3, 1)
                for k in range(5):
                    cur, nxt = k % 2, 1 - (k % 2)
                    need_sq = k <= 3
                    need_q = k >= 1
                    for h in range(4):
                        hsl = slice(h * 128, (h + 1) * 128)
                        if need_sq:
                            mm(B0[:, hsl], W[cur][:, h, :], VP[cur][:, h, 0, :], True, True, [bW[cur], bV[cur]], [p0], inc=(h == 3))
                    for h in range(4):
                        hsl = slice(h * 128, (h + 1) * 128)
                        if need_sq:
                            mm(B2[:, hsl], VP[cur][:, h, 0, :], W[cur][:, h, :], True, True, [bW[cur], bV[cur]], [p2b], inc=(h == 3))
                    for h in range(4):
                        hsl = slice(h * 128, (h + 1) * 128)
                        if need_q:
                            mm(B1[:, hsl], W[cur][:, h, :], VP[cur][:, h, 1, :], True, False, [bW[cur], bP[cur]], [p1], inc=False)
                            mm(B1[:, hsl], identb, VP[cur][:, h, 1, :], False, True, [b_cmb, bP[cur]], [p1], inc=(h == 3))
                    for h in range(4):
                        hsl = slice(h * 128, (h + 1) * 128)
                        if need_q:
                            mm(B3[:, hsl], VP[cur][:, h, 1, :], W[cur][:, h, :], True, False, [bW[cur], bP[cur]], [p3], inc=False)
                            mm(B3[:, hsl], VP[cur][:, h, 1, :], identb, False, True, [b_cmb, bP[cur]], [p3], inc=(h == 3))
                    if need_sq:
                        act(VP[nxt][:, :, 0, :], v3(B0), AF.Copy, [p0], [bV[nxt]])
                    if need_q:
                        cp("dve", VP[nxt][:, :, 1, :], v3(B1), [p1], [bP[nxt]])
                    if need_sq:
                        act(W[nxt], v3(B2), AF.Copy, [p2b], [bW[nxt]])
                    if need_q:
                        cp("act" if k % 2 == 1 else "dve", PT[nxt], v3(B3), [p3], [bPT[nxt]])
                    yield
                Xd, XdT = VP[1][:, :, 1, :], PT[1]
                for h in range(4):
                    mm(B0[:, h * 128:(h + 1) * 128], OFF[:, h, 1, :], Xd[:, h, :], True, True, [bOFF, bP[1]], [p0], inc=(h == 3))
                for h in range(4):
                    mm(B1[:, h * 128:(h + 1) * 128], OFF[:, h, 0, :], XdT[:, h, :], True, True, [bOFF, bPT[1]], [p1], inc=(h == 3))
                act(Y[:, 0, :, :], v3(B0), AF.Copy, [p0], [bE])
                cp("dve", Y[:, 1, :, :], v3(B1), [p1], [bE])
                yield
                for h in range(4):
                    hsl = slice(h * 128, (h + 1) * 128)
                    mm(B2[:, hsl], XdT[:, h, :], Y[:, 0, h, :], True, False, [bPT[1], bE], [p2b], inc=False)
                    mm(B2[:, hsl], XdT[:, h, :], identb, False, True, [bPT[1], b_cmb], [p2b], inc=(h == 3))
                for h in range(4):
                    hsl = slice(h * 128, (h + 1) * 128)
                    mm(B3[:, hsl], Xd[:, h, :], Y[:, 1, h, :], True, False, [bP[1], bE], [p3], inc=False)
                    mm(B3[:, hsl], Xd[:, h, :], identb, False, True, [bP[1], b_cmb], [p3], inc=(h == 3))
                X64, X64T = VP[0][:, :, 1, :], PT[0]
                act(X64, v3(B2), AF.Copy, [p2b], [bP[0]])
                cp("dve", X64T, v3(B3), [p3], [bPT[0]])
                yield
                for h in range(4):
                    mm(B0[:, h * 128:(h + 1) * 128], OFF[:, h, 2, :], X64[:, h, :], True, True, [bOFF, bP[0]], [p0], inc=(h == 3))
                act(Y[:, 0, :, :], v3(B0), AF.Copy, [p0], [bE])
                for h in range(4):
                    hsl = slice(h * 128, (h + 1) * 128)
                    mm(B1[:, hsl], X64T[:, h, :], Y[:, 0, h, :], True, False, [bPT[0], bE], [p1], inc=False)
                    mm(B1[:, hsl], X64T[:, h, :], identb, False, True, [bPT[0], b_cmb], [p1], inc=(h == 3))
                cp("dve", Z["T2T"], v3(B1), [p1], [Z["b_T2T"]])
                yield
                f64 = [128, 4, 64]
                for h in range(4):
                    hd = 4 * hf + h
                    mm(B3[:, 256 + h * 64:256 + (h + 1) * 64], Z["T2T"][:, h, :], v_tok[:, n, hd * 64:(hd + 1) * 64], True, True,
                       [Z["b_T2T"], b_vt[n]], [p3], inc=(h == 3))
                for q2 in range(2):
                    for hh in range(2):
                        mm(B0[:, (q2 * 2 + hh) * 128:(q2 * 2 + hh + 1) * 128],
                           Z["kg"][:, 2 * q2:2 * q2 + 2, :].rearrange("p a c -> p (a c)"), Z["T2T"][:, 2 * q2 + hh, :], True, True,
                           [Z["b_kg"], Z["b_T2T"]], [p0], inc=(q2 == 1 and hh == 1))
                act(Z["usb"][:], B3[:, 256:512].rearrange("p (h c) -> p h c", h=4), AF.Copy, [p3], [Z["b_usb"]])
                b04 = B0.rearrange("p (q a c) -> p q a c", q=2, a=2)
                act(Z["wT"][0:64, :, :], b04[0:64, :, 0, :], AF.Copy, [p0], [Z["b_wT"]])
                act(Z["wT"][64:128, :, :], b04[64:128, :, 1, :], AF.Copy, [p0], [Z["b_wT"]])
                yield
                sm, sbd, bsm, bsbd = Sm[d][hf], Sbd[d][hf], b_Sm[d][hf], b_Sbd[d][hf]
                first = (n % 2 == 0) if d == 0 else (n % 2 == 1)
                last = not first
                for q2 in range(2):
                    mm(B1[:, q2 * 128:(q2 + 1) * 128], Z["wT"][:, q2, :], sbd[:, q2, :], True, True, [Z["b_wT"], bsbd], [p1],
                       inc=False)
                for q2 in range(2):
                    mm(B1[:, 256 + q2 * 128:256 + (q2 + 1) * 128], qT[:, 2 * hf + q2, tl], sbd[:, q2, :], True, True,
                       [b_qT[n], bsbd], [p1], inc=(q2 == 1))
                tt("dve", Z["tmp"][:], B1[:, 0:256].rearrange("p (h c) -> p h c", h=4), Z["usb"][:], ALU.subtract,
                   [p1, Z["b_usb"]], [Z["b_tmp"]])
                tt("dve", Z["vnb"][:], Z["tmp"][:], bc(nbet[:, n, hs], f64, 2), ALU.mult, [Z["b_tmp"], b_nbet], [Z["b_vnb"]])
                for h in range(4):
                    mm(B2[:, h * 64:(h + 1) * 64], Z["attnT"][:, h, :], Z["vnb"][:, h, :], True, True,
                       [Z["b_attnT"], Z["b_vnb"]], [p2b], inc=False)
                for q2 in range(2):
                    mm(B2[:, 256 + q2 * 128:256 + (q2 + 1) * 128], Z["kt"][:, 2 * q2:2 * q2 + 2, :].rearrange("p a c -> p (a c)"),
                       Z["vnb"][:, 2 * q2:2 * q2 + 2, :].rearrange("p a c -> p (a c)"), True, True,
                       [Z["b_kt"], Z["b_vnb"]], [p2b], inc=(q2 == 1))
                yield
                ps4 = B2[:, 256:512].rearrange("p (q c) -> p q c", q=2)
                tt("dve", sm[:], sm[:], bc(eglp[:, n, d, 2 * hf:2 * hf + 2], [128, 2, 64], 2), ALU.mult, [bsm, b_eglp], [bsm])
                tt("dve", sm[0:64, :, :], sm[0:64, :, :], ps4[0:64, :, 0:64], ALU.add, [bsm, p2b], [bsm])
                tt("dve", sm[64:128, :, :], sm[64:128, :, :], ps4[64:128, :, 64:128], ALU.add, [bsm, p2b], [bsm])
                cp("act", sbd[0:64, :, 0:64], sm[0:64, :, :], [bsm], [bsbd])
                cp("act", sbd[64:128, :, 64:128], sm[64:128, :, :], [bsm], [bsbd])
                tt("dve", Z["o1"][:], B1[:, 256:512].rearrange("p (h c) -> p h c", h=4), bc(egc[:, n, hs], f64, 2), ALU.mult,
                   [p1, b_egc], [Z["b_o1"]])
                tt("dve", Z["o1"][:], B2[:, 0:256].rearrange("p (h c) -> p h c", h=4), Z["o1"][:], ALU.add,
                   [p2b, Z["b_o1"]], [Z["b_o1"]])
                osl = oacc[:, n, 256 * hf:256 * hf + 256].rearrange("p (h c) -> p h c", h=4)
                tt("pool", osl, osl, Z["o1"][:], ALU.add, [Z["b_o1"], b_oacc[n][hf]], [b_oacc[n][hf]])
                if last:
                    kb.dma("sp", s_out[d, n // 2, 2 * hf:2 * hf + 2].rearrange("q p e -> p q e"), sm[:], reads=[bsm], track=b_sout[d][hf])
                yield

            ycount = [0]

            def run_pair(g0, g1):
                gens = [g0, g1]
                while gens:
                    for g in list(gens):
                        try:
                            if next(g) == "stop":
                                return True
                            ycount[0] += 1
                            if 30 <= stage < 60 and ycount[0] >= 2 * (stage - 30 + 1):
                                return True
                        except StopIteration:
                            gens.remove(g)
                return False

            nsteps = NT if stage != 3 else 2
            for st in range(nsteps):
                for d in range(2):
                    n = st if d == 0 else NT - 1 - st
                    if run_pair(problem(d, n, 0), problem(d, n, 1)):
                        break
                if 30 <= stage < 70:
                    break

            if stage in (3, 4) or 30 <= stage < 70:
                kb.barrier()
                dump(oacc[:, 0, :], 512, b_oacc[0])
                dump(oacc[:, 1, :], 512, b_oacc[1])
                dump(oacc[:, 15, :], 512, b_oacc[15])
                dump(oacc[:, 7, :], 512, b_oacc[7])
                dump(Sm[0][0][:], 128, [b_Sm[0][0]])
                dump(Sm[1][1][:], 128, [b_Sm[1][1]])
                dump(S[0]["T2T"][:], 512, [S[0]["b_T2T"]])
                dump(S[0]["attnT"][:], 512, [S[0]["b_attnT"]])
                dump(S[0]["usb"][:], 256, [S[0]["b_usb"]])
                dump(S[0]["wT"][:], 256, [S[0]["b_wT"]])
                dump_done()
                return nc
            kb.barrier()

            onr = T("onr", [128, 64]); b_onr = kb.buf("onr")
            kb.dma("sp", onr[:], onorm.partition_broadcast(128), writes=[b_onr])
            g_sq = [S[i]["Em"].rearrange("p h c -> p (h c)") for i in range(2)]
            g_on = [S[i]["E"].rearrange("p h c -> p (h c)") for i in range(2)]
            g_ob = [S[i]["OFF"].rearrange("p a b c -> p (a b c)")[:, 0:512] for i in range(2)]
            g_st = [S[i]["usb"].rearrange("p h c -> p (h c)") for i in range(2)]
            bg_sq, bg_on, bg_ob, bg_st = [kb.bufs(nm, 2) for nm in ("gsq", "gon", "gob", "gst")]
            def it_gate(n):
                i2 = n % 2
                o3 = oacc[:, n, :].rearrange("p (h c) -> p h c", h=8)
                sq3 = g_sq[i2].rearrange("p (h c) -> p h c", h=8)
                on3 = g_on[i2].rearrange("p (h c) -> p h c", h=8)
                rdo = b_oacc[n]
                tt("dve", g_sq[i2], oacc[:, n, :], oacc[:, n, :], ALU.mult, rdo, [bg_sq[i2]])
                yield
                op("dve", lambda e, i2=i2, sq3=sq3: e.tensor_reduce(out=g_st[i2][:, 0:8], in_=sq3, axis=AX.X, op=ALU.add),
                   [bg_sq[i2]], [bg_st[i2]])
                yield
                act(g_st[i2][:, 8:16], g_st[i2][:, 0:8], AF.Sqrt, [bg_st[i2], b_eps], [bg_st[i2]], scale=1.0 / 64, bias=eps_t[:, 0:1])
                yield
                op("dve", lambda e, i2=i2: e.reciprocal(out=g_st[i2][:, 16:24], in_=g_st[i2][:, 8:16]), [bg_st[i2]], [bg_st[i2]])
                tt("dve", on3, o3, bc(g_st[i2][:, 16:24], [128, 8, 64], 2), ALU.mult, rdo + [bg_st[i2]], [bg_on[i2]])
                yield
                tt("dve", on3, on3, bc(onr[:], [128, 8, 64], 1), ALU.mult, [bg_on[i2], b_onr], [bg_on[i2]])
                tt("dve", g_ob[i2], g_on[i2], zs[:, n, :], ALU.mult, [bg_on[i2], b_zs[n]], [bg_ob[i2]])
                yield
                bk = i2
                for c in range(4):
                    mm(PS[:, bk, c * 128:(c + 1) * 128], g_ob[i2][:, c * 128:(c + 1) * 128], identb, True, True,
                       [bg_ob[i2], b_cmb], [pb[bk]], inc=(c == 3))
                yield
                cp("act", onT[:, :, n * 128:(n + 1) * 128], PS[:, bk, :].rearrange("p (c t) -> p c t", c=4), [pb[bk]], [b_onT[n]])
                yield
            pipeline((it_gate(n) for n in range(NT)), 2)

            if stage == 5:
                kb.barrier()
                dump(onT[:, 0, 0:1024], 1024, b_onT)
                dump(onT[:, 3, 1024:2048], 1024, b_onT)
                dump_done()
                return nc
            kb.barrier()
            kb.flush()

        if True:
            x1 = Region(56 * KB_, 88 * KB_).take([128, 8, DM])
            h2T = Region(88 * KB_, 104 * KB_).take([128, 8, 1024], BF16)
            y2a = Region(88 * KB_, 104 * KB_).take([128, 8, 512])
            R3 = Region(D0, AR_BYTES)
            WBH['wb'] = [R3.take([128, 8, 512], BF16) for i in range(NWB)]
            wb_ctr[0] = 0
            b_wb3 = kb.bufs("wb3_", 3)
            for i in range(3):
                b_wb[i] = b_wb3[i]
            D3 = R3.pos
            hTh = Region(D3, D3 + 16 * KB_).take([128, 8, 1024], BF16)
            mT = Region(D3 + 16 * KB_, D3 + 32 * KB_).take([128, 8, 1024], BF16)
            actT = Region(D3, D3 + 44 * KB_).take([128, 22, 1024], BF16)
            R3s = Region(D3 + 44 * KB_, AR_BYTES)
            t1p2 = [R3s.take([128, DM]) for i in range(2)]
            t1p = t1p2[0]
            htk = [R3s.take([128, DM], BF16) for i in range(2)]
            sga = [R3s.take([128, 512]) for i in range(2)]
            sgb = [R3s.take([128, 512]) for i in range(2)]
            st3 = R3s.take([128, 64])
            b_x1 = kb.bufs("x1_", 8); b_h2T = kb.bufs("h2T_", 8); b_y2a = kb.bufs("y2a_", 8)
            b_hTh = kb.bufs("hTh_", 8); b_mT = kb.bufs("mT_", 8); b_actT = kb.bufs("actT_", 2)
            b_t1p2 = kb.bufs("t1p", 2); b_t1p = b_t1p2[0]; b_htk = kb.bufs("htk", 2); b_sga = kb.bufs("sga", 2); b_sgb = kb.bufs("sgb", 2)
            b_st3 = kb.bufs("st3_", 64)
            b_yout = kb.bufs("yout", 8)

            def norm_transpose(src_tile, rd, Ar, Br, bA, bB, dstT, bdst, j, scol, pre=None):
                i2 = j % 2
                c0, c1, c2 = scol, scol + 1, scol + 2
                tp_, btp = t1p2[i2], b_t1p2[i2]
                if pre is not None:
                    pre()
                act(htk[i2], src_tile, AF.Square, rd, [b_htk[i2], b_st3[c0]], accum_out=st3[:, c0:c0 + 1])
                act(st3[:, c1:c1 + 1], st3[:, c0:c0 + 1], AF.Sqrt, [b_st3[c0], b_eps], [b_st3[c1]], scale=1.0 / DM, bias=eps_t[:, 0:1])
                yield
                op("dve", lambda e: e.reciprocal(out=st3[:, c2:c2 + 1], in_=st3[:, c1:c1 + 1]), [b_st3[c1]], [b_st3[c2]])
                stt("dve", tp_, src_tile, st3[:, c2:c2 + 1], Ar, ALU.mult, ALU.mult, rd + [b_st3[c2], bA], [btp])
                yield
                tt("dve", htk[i2], tp_, Br, ALU.add, [btp, bB], [b_htk[i2]])
                yield
                bk = 2 * i2
                for kc in range(8):
                    mm(PS[:, bk + kc // 4, (kc % 4) * 128:(kc % 4 + 1) * 128], htk[i2][:, kc * 128:(kc + 1) * 128], identb,
                       True, True, [b_htk[i2], b_cmb], [pb[bk + kc // 4]], inc=(kc % 4 == 3))
                yield
                cp("act", dstT[:, :, j * 128:(j + 1) * 128],
                   PS[:, bk:bk + 2, :].rearrange("p a (c t) -> p (a c) t", t=128), [pb[bk], pb[bk + 1]], [bdst])
                yield

            for half in range(2):
                T0_ = half * 1024
                def ld(j):
                    return lambda: kb.dma("sp", x1[:, j, :], x[T0_ + j * 128:T0_ + (j + 1) * 128, :], writes=[b_x1[j]])
                for j in range(8):
                    ld(j)()
                pipeline((norm_transpose(x1[:, j, :], [b_x1[j]], A1r, B1r, b_modr[1], b_modr[0], hTh, b_hTh[j], j, 3 * j)
                          for j in range(8)), 2)
                if stage == 70:
                    kb.barrier()
                    dump(hTh[:, 0, :], 1024, b_hTh)
                    dump_done()
                    return nc
                def it_a2(c, t2, rot, wt, bw):
                    i2 = rot % 2
                    bs = (0, 1, 2, 3) if i2 == 0 else (4, 5, 6, 7)
                    hsl = slice(t2 * 512, (t2 + 1) * 512)
                    gsl = slice(T0_ + t2 * 512, T0_ + (t2 + 1) * 512)
                    rdh = b_hTh[4 * t2:4 * t2 + 4]
                    gt0 = (T0_ + t2 * 512) // 128
                    for kc in range(8):
                        mm(PS[:, bs[0], :], wt[:, kc, 0:128], hTh[:, kc, hsl], kc == 0, kc == 7, [bw] + rdh, [pb[bs[0]]])
                    for kc in range(8):
                        mm(PS[:, bs[1], :], wt[:, kc, 128:256], hTh[:, kc, hsl], kc == 0, kc == 7, [bw] + rdh, [pb[bs[1]]])
                    for kc in range(4):
                        mm(PS[:, bs[2], :], wt[:, kc, 256:384], onT[:, kc, gsl], kc == 0, kc == 3, [bw] + b_onT[gt0:gt0 + 4], [pb[bs[2]]])
                    for kc in range(4):
                        mm(PS[:, bs[3], :], wt[:, kc, 384:512], ysc[:, kc, gsl], kc == 0, kc == 3, [bw] + b_ysc[gt0:gt0 + 4], [pb[bs[3]]])
                    yield
                    act(sga[i2], PS[:, bs[0], :], AF.Sigmoid, [pb[bs[0]]], [b_sga[i2]])
                    act(sgb[i2], PS[:, bs[1], :], AF.Sigmoid, [pb[bs[1]]], [b_sgb[i2]])
                    yield
                    tt("dve", sga[i2], PS[:, bs[2], :], sga[i2], ALU.mult, [pb[bs[2]], b_sga[i2]], [b_sga[i2]])
                    tt("dve", sgb[i2], PS[:, bs[3], :], sgb[i2], ALU.mult, [pb[bs[3]], b_sgb[i2]], [b_sgb[i2]])
                    yield
                    tt("dve", mT[:, c, hsl], sga[i2], sgb[i2], ALU.add, [b_sga[i2], b_sgb[i2]], b_mT[4 * t2:4 * t2 + 4])
                    yield

                def gen_a2():
                    rot = 0

                    def ld_a2(c):
                        csl = slice(c * 128, (c + 1) * 128)
                        wt, bw = wload(None, parts=[(w_in[:, GA0 + c * 128:GA0 + (c + 1) * 128], 128),
                                                    (w_in[:, GB0 + c * 128:GB0 + (c + 1) * 128], 128)])
                        kb.dma("pool", wt[:, 0:4, 256:384], w_br_a[:, csl].rearrange("(kc p) c -> p kc c", p=128), writes=[bw], partial=True)
                        kb.dma("pool", wt[:, 0:4, 384:512], w_br_b[:, csl].rearrange("(kc p) c -> p kc c", p=128), writes=[bw], partial=True)
                        return wt, bw
                    getw = prefetched([(lambda c=c: ld_a2(c)) for c in range(8)])
                    for c in range(8):
                        wt, bw = getw(c)
                        for t2 in range(2):
                            yield it_a2(c, t2, rot, wt, bw)
                            rot += 1
                pipeline(gen_a2(), 2)
                if stage == 71:
                    kb.barrier()
                    dump(mT[:, 0, :], 1024, b_mT)
                    dump_done()
                    return nc
                wo = []
                for nb in range(2):
                    wo.append(wload(w_out[:, nb * 512:(nb + 1) * 512]))
                def it_a3(j):
                    i2 = j % 2
                    bk = 2 * i2 + 4
                    tp_, btp = t1p2[i2], b_t1p2[i2]
                    for nb in range(2):
                        for kc in range(8):
                            mm(PS[:, bk + nb, :], mT[:, kc, j * 128:(j + 1) * 128], wo[nb][0][:, kc, :], kc == 0, kc == 7,
                               [b_mT[j], wo[nb][1]], [pb[bk + nb]])
                    yield
                    ps2 = PS[:, bk:bk + 2, :].rearrange("p a b -> p (a b)")
                    rd = [pb[bk], pb[bk + 1]]
                    c0 = 24 + 3 * j
                    act(htk[i2], ps2, AF.Square, rd, [b_htk[i2], b_st3[c0]], accum_out=st3[:, c0:c0 + 1])
                    act(st3[:, c0 + 1:c0 + 2], st3[:, c0:c0 + 1], AF.Sqrt, [b_st3[c0], b_eps], [b_st3[c0 + 1]], scale=1.0 / DM, bias=eps_t[:, 0:1])
                    yield
                    op("dve", lambda e, c0=c0: e.reciprocal(out=st3[:, c0 + 2:c0 + 3], in_=st3[:, c0 + 1:c0 + 2]), [b_st3[c0 + 1]], [b_st3[c0 + 2]])
                    stt("dve", tp_, ps2, st3[:, c0 + 2:c0 + 3], C1r, ALU.mult, ALU.mult, rd + [b_st3[c0 + 2], b_modr[2]], [btp])
                    yield
                    tt("dve", x1[:, j, :], x1[:, j, :], tp_, ALU.add, [b_x1[j], btp], [b_x1[j]])
                    yield
                pipeline((it_a3(j) for j in range(8)), 2)
                pipeline((norm_transpose(x1[:, j, :], [b_x1[j]], A2r, B2r, b_modr[4], b_modr[3], h2T, b_h2T[j], j, 3 * j)
                          for j in range(8)), 2)
                kb.barrier()
                if stage == 73:
                    kb.barrier()
                    dump(h2T[:, 0, :], 1024, b_h2T)
                    dump_done()
                    return nc
                def it_b1(c, cc, t2, rot, wt, bw):
                    i2 = rot % 2
                    bs = (0, 1) if i2 == 0 else (2, 3)
                    hsl = slice(t2 * 512, (t2 + 1) * 512)
                    rdh = b_h2T[4 * t2:4 * t2 + 4]
                    for kc in range(8):
                        mm(PS[:, bs[0], :], wt[:, kc, cc * 128:(cc + 1) * 128], h2T[:, kc, hsl], kc == 0, kc == 7, [bw] + rdh, [pb[bs[0]]])
                    for kc in range(8):
                        mm(PS[:, bs[1], :], wt[:, kc, 256 + cc * 128:256 + (cc + 1) * 128], h2T[:, kc, hsl], kc == 0, kc == 7, [bw] + rdh, [pb[bs[1]]])
                    yield
                    act(sga[i2], PS[:, bs[0], :], AF.Silu, [pb[bs[0]]], [b_sga[i2]])
                    yield
                    tt("dve", actT[:, c, hsl], PS[:, bs[1], :], sga[i2], ALU.mult, [pb[bs[1]], b_sga[i2]], [b_actT[t2]])
                    yield

                def gen_b1():
                    rot = 0
                    getw = prefetched([(lambda c2=c2: wload(None, parts=[(w_ffn_in[:, c2 * 256:(c2 + 1) * 256], 256),
                                                                        (w_ffn_in[:, DFF + c2 * 256:DFF + (c2 + 1) * 256], 256)]))
                                       for c2 in range(11)])
                    for c2 in range(11):
                        wt, bw = getw(c2)
                        for cc in range(2):
                            for t2 in range(2):
                                yield it_b1(2 * c2 + cc, cc, t2, rot, wt, bw)
                                rot += 1
                pipeline(gen_b1(), 2)
                kb.barrier()
                if stage == 74:
                    kb.barrier()
                    dump(actT[:, 0, :], 1024, b_actT)
                    dump_done()
                    return nc
                getw2 = prefetched([(lambda ch=ch, g8=g8: wload(w_ffn_out[g8 * 1024:g8 * 1024 + (8 if g8 < 2 else 6) * 128,
                                                                          ch * 512:(ch + 1) * 512]))
                                    for ch in range(2) for g8 in range(3)])
                for ch in range(2):
                    for g8 in range(3):
                        nk = 8 if g8 < 2 else 6
                        wt, bw = getw2(ch * 3 + g8)
                        for kl in range(nk):
                            kc = g8 * 8 + kl
                            for j in range(8):
                                mm(PS[:, j, :], actT[:, kc, j * 128:(j + 1) * 128], wt[:, kl, :], kc == 0, kc == 21,
                                   [bw, b_actT[j // 4]], [pb[j]], inc=(kc == 21) or (kl == nk - 1 and j == 7))
                    if stage == 75:
                        kb.barrier()
                        dump(PS[:, 0, :], 512, [pb[0]])
                        dump_done()
                        return nc
                    def it_b2e(j, ch=ch):
                        c0 = 48 + 2 * j
                        i2 = j % 2
                        tp_, btp = t1p2[i2], b_t1p2[i2]
                        if ch == 0:
                            act(htk[i2][:, 0:512], PS[:, j, :], AF.Square, [pb[j]], [b_htk[i2], b_st3[c0]], accum_out=st3[:, c0:c0 + 1])
                            cp("act", y2a[:, j, :], PS[:, j, :], [pb[j]], [b_y2a[j]])
                            yield
                        else:
                            act(htk[i2][:, 0:512], PS[:, j, :], AF.Square, [pb[j]], [b_htk[i2], b_st3[c0 + 1]], accum_out=st3[:, c0 + 1:c0 + 2])
                            yield
                            tt("dve", st3[:, c0:c0 + 1], st3[:, c0:c0 + 1], st3[:, c0 + 1:c0 + 2], ALU.add, [b_st3[c0], b_st3[c0 + 1]], [b_st3[c0]])
                            yield
                            act(st3[:, c0 + 1:c0 + 2], st3[:, c0:c0 + 1], AF.Sqrt, [b_st3[c0], b_eps], [b_st3[c0 + 1]], scale=1.0 / DM, bias=eps_t[:, 0:1])
                            yield
                            op("dve", lambda e, c0=c0: e.reciprocal(out=st3[:, c0:c0 + 1], in_=st3[:, c0 + 1:c0 + 2]), [b_st3[c0 + 1]], [b_st3[c0]])
                            stt("dve", tp_[:, 0:512], y2a[:, j, :], st3[:, c0:c0 + 1], C2r[:, 0:512], ALU.mult, ALU.mult,
                                [b_y2a[j], b_st3[c0], b_modr[5]], [btp])
                            stt("dve", tp_[:, 512:1024], PS[:, j, :], st3[:, c0:c0 + 1], C2r[:, 512:1024], ALU.mult, ALU.mult,
                                [pb[j], b_st3[c0], b_modr[5]], [btp])
                            yield
                            tt("dve", x1[:, j, :], x1[:, j, :], tp_, ALU.add, [b_x1[j], btp], [b_x1[j]])
                            yield
                            kb.dma("sp", y[T0_ + j * 128:T0_ + (j + 1) * 128, :], x1[:, j, :], reads=[b_x1[j]], track=b_yout[j])
                            yield
                    pipeline((it_b2e(j) for j in range(8)), 2)
                    if stage == 76:
                        kb.barrier()
                        dump(y2a[:, 0, :], 512, [b_y2a[0]])
                        dump_done()
                        return nc
                kb.barrier()
                if stage == 77:
                    dump(x1[:, 0, :], 1024, [b_x1[0]])
                    dump_done()
                    return nc

        if stage == 6:
            dump(x1[:, 0, :], 1024, [b_x1[0]])
            dump_done()
            return nc
        kb.finish()
    return nc


def _const_mats():
    a = np.arange(128)[:, None]
    b = np.arange(128)[None, :]
    m = np.zeros((NMAT, 128, 128), np.float32)
    m[IDENT] = (a == b)
    m[MFI] = (a <= b)
    m[MFS] = (a < b)
    m[MBI] = (a >= b)
    m[MBS] = (a > b)
    m[BONES] = ((a // 64) == (b // 64))
    m[ONES] = 1.0
    m[BD32] = ((a // 32) == (b // 32)) & (a != b)
    m[NEGTF] = np.where(b <= a, 0.0, -30000.0)
    m[NEGTB] = np.where(b >= a, 0.0, -30000.0)
    m[M1I] = ((a // 64) == (b // 64)) & ((a // 32) != (b // 32))
    m[M2I] = ((a // 64) != (b // 64))
    return m


def make_in_maps(inp):
    f = lambda a: np.ascontiguousarray(np.asarray(a, dtype=np.float32))
    shared = dict(
        w_mod=f(inp["w_mod"][0]), b_mod=f(inp["b_mod"][0]), norm_gains=f(inp["norm_gains"][0]),
        w_in=f(inp["w_in"][0]),
        cwq=f(np.asarray(inp["conv_qkv"][0]).reshape(3, 12, 128).transpose(2, 1, 0)),
        cws=f(np.asarray(inp["conv_sc"][0]).reshape(3, 4, 128).transpose(2, 1, 0)),
        a_log=f(np.asarray(inp["a_log"][0]).reshape(16)), dt_bias=f(np.asarray(inp["dt_bias"][0]).reshape(16)),
        onorm=f(inp["onorm"][0]), w_br_a=f(inp["w_branch_a"][0]), w_br_b=f(inp["w_branch_b"][0]),
        w_out=f(inp["w_out"][0]), w_ffn_in=f(inp["w_ffn_in"][0]), w_ffn_out=f(inp["w_ffn_out"][0]),
        cmat=_const_mats(),
    )
    maps = []
    xs = np.asarray(inp["x_sample"]); xp = np.asarray(inp["x_prompt"])
    sd = np.asarray(inp["state_delta"]); c = np.asarray(inp["c"]); cc = np.asarray(inp["c_ctx"])
    for b in range(4):
        m = dict(shared)
        m["x"] = f(xs[b])
        m["cvec"] = f(c[b].reshape(8, 128).T)
        fl = np.zeros((128, 2), np.float32); fl[:, 0] = 1.0
        m["flags"] = fl
        ini = np.zeros((2, 8, 4, 128, 64), np.float32)
        ini[0, 0] = sd[b, 0, 0].reshape(4, 128, 64)
        ini[1, 7] = sd[b, 0, 1].reshape(4, 128, 64)
        m["init_s"] = ini
        maps.append(m)
    for j in range(4):
        m = dict(shared)
        m["x"] = f(xp[8 * j:8 * j + 8].reshape(TOK, DM))
        m["cvec"] = f(cc.reshape(8, 128).T)
        fl = np.zeros((128, 2), np.float32); fl[:, 1] = 1.0
        m["flags"] = fl
        m["init_s"] = np.zeros((2, 8, 4, 128, 64), np.float32)
        maps.append(m)
    return maps


_NC_CACHE = {}


def kernel(**inputs):
    if "nc" not in _NC_CACHE:
        _NC_CACHE["nc"] = build_program()
    nc = _NC_CACHE["nc"]
    maps = make_in_maps(inputs)
    res = run_bass_kernel_spmd(nc, maps, core_ids=list(range(8))).results
    y_s = np.stack([res[b]["y"] for b in range(4)], axis=0).astype(np.float32)
    y_p = np.concatenate([res[4 + j]["y"].reshape(8, 256, DM) for j in range(4)], axis=0).astype(np.float32)
    st = np.zeros((32, 1, 2, 8, 64, 64), np.float32)
    for j in range(4):
        so = res[4 + j]["s_out"]
        st[8 * j:8 * j + 8, 0] = so.reshape(2, 8, 8, 64, 64).transpose(1, 0, 2, 3, 4)
    return (y_p, y_s, st)
```
